# Optimizing a Trainium2 kernel written in Bass

```python
import math
import jax, jax.numpy as jnp
from jax import lax
import numpy as np

D_MODEL = 1024
BATCH = 4
SEQ = 8192
DEPTH = 1

ATTN_HEADS = 8
ATTN_HEAD_DIM = 64
ATTN_WIDTH = ATTN_HEADS * ATTN_HEAD_DIM
MOBA_BLOCK = 256
MOBA_TOPK = 3
MOBA_Q_BLOCK = 64
REL_BUCKETS = 32
REL_MAX_DISTANCE = 128
DN_HEADS = 4
DN_HEAD_K = 128
DN_HEAD_V = 128
DN_QK_WIDTH = DN_HEADS * DN_HEAD_K
DN_V_WIDTH = DN_HEADS * DN_HEAD_V
DN_CONV_WIDTH = 4
DN_CONV_CH = 2 * DN_QK_WIDTH + DN_V_WIDTH
DN_CHUNK = 64
D_MIX = ATTN_WIDTH + DN_V_WIDTH
IN_COLS = 4 * ATTN_WIDTH + DN_CONV_CH + DN_V_WIDTH + 2 * DN_HEADS
EPS = 1e-6

kernel_name = "hymba_moba_gated_deltanet_layer"


def rms_norm(x, w):
    xf = x.astype(jnp.float32)
    y = xf * lax.rsqrt(jnp.mean(xf * xf, axis=-1, keepdims=True) + EPS)
    return (y * w.astype(jnp.float32)).astype(x.dtype)


def l2_norm(x):
    xf = x.astype(jnp.float32)
    return xf * lax.rsqrt(jnp.sum(xf * xf, axis=-1, keepdims=True) + EPS)


def t5_bucket(dist):
    n = jnp.maximum(dist, 0)
    max_exact = REL_BUCKETS // 2
    nf = jnp.maximum(n, 1).astype(jnp.float32)
    large = max_exact + (jnp.log(nf / max_exact) / math.log(REL_MAX_DISTANCE / max_exact)
                         * (REL_BUCKETS - max_exact)).astype(jnp.int32)
    large = jnp.minimum(large, REL_BUCKETS - 1)
    return jnp.where(n < max_exact, n, large)


def causal_depthwise_conv(u, w):
    k_w, ch = w.shape
    return lax.conv_general_dilated(u, w[:, None, :], window_strides=(1,), padding=[(k_w - 1, 0)],
                                    dimension_numbers=('NWC', 'WIO', 'NWC'), feature_group_count=ch)


def moba_attention(q, k, v, rel_bias):
    B, H, S, D = q.shape
    nb = -(-S // MOBA_BLOCK)
    s_pad = nb * MOBA_BLOCK
    padw = ((0, 0), (0, 0), (0, s_pad - S), (0, 0))
    q, k, v = jnp.pad(q, padw), jnp.pad(k, padw), jnp.pad(v, padw)
    k_blocks = k.reshape(B, H, nb, MOBA_BLOCK, D)
    v_blocks = v.reshape(B, H, nb, MOBA_BLOCK, D)
    n_gate = max(nb, MOBA_TOPK)
    k_mean = jnp.mean(k_blocks.astype(jnp.float32), axis=3)
    k_mean = jnp.pad(k_mean, ((0, 0), (0, 0), (0, n_gate - nb), (0, 0)))
    scale = D ** -0.5
    b_ix = jnp.arange(B)[:, None, None, None]
    h_ix = jnp.arange(H)[None, :, None, None]
    blk_ar = jnp.arange(MOBA_BLOCK)
    nqb = s_pad // MOBA_Q_BLOCK
    q_sweep = jnp.moveaxis(q.reshape(B, H, nqb, MOBA_Q_BLOCK, D), 2, 0)

    def one_query_block(args):
        qb, q_blk = args
        q_pos = qb * MOBA_Q_BLOCK + jnp.arange(MOBA_Q_BLOCK)
        own = (qb * MOBA_Q_BLOCK) // MOBA_BLOCK
        gate = jnp.einsum('bhqd,bhnd->bhqn', q_blk.astype(jnp.float32), k_mean)
        gate = jnp.where(jnp.arange(n_gate) < own, gate, -jnp.inf)
        _, idx = lax.top_k(gate, MOBA_TOPK)
        valid = idx < own
        idx = jnp.minimum(idx, nb - 1)
        k_sel = k_blocks[b_ix, h_ix, idx]
        v_sel = v_blocks[b_ix, h_ix, idx]
        s_sel = jnp.einsum('bhqd,bhqsjd->bhqsj', q_blk, k_sel).astype(jnp.float32) * scale
        dist_sel = q_pos[None, None, :, None, None] - (idx[..., None] * MOBA_BLOCK + blk_ar)
        s_sel = s_sel + rel_bias[t5_bucket(dist_sel), h_ix[..., None]]
        s_sel = jnp.where(valid[..., None], s_sel, -jnp.inf)
        k_own = lax.dynamic_index_in_dim(k_blocks, own, axis=2, keepdims=False)
        v_own = lax.dynamic_index_in_dim(v_blocks, own, axis=2, keepdims=False)
        dist_own = q_pos[:, None] - (own * MOBA_BLOCK + blk_ar)[None, :]
        s_own = (jnp.einsum('bhqd,bhjd->bhqj', q_blk, k_own).astype(jnp.float32) * scale
                 + jnp.moveaxis(rel_bias[t5_bucket(dist_own)], -1, 0))
        s_own = jnp.where(dist_own >= 0, s_own, -jnp.inf)
        logits = jnp.concatenate([s_own, s_sel.reshape(B, H, MOBA_Q_BLOCK, MOBA_TOPK * MOBA_BLOCK)], axis=-1)
        p = jax.nn.softmax(logits, axis=-1).astype(v.dtype)
        p_own = p[..., :MOBA_BLOCK]
        p_sel = p[..., MOBA_BLOCK:].reshape(B, H, MOBA_Q_BLOCK, MOBA_TOPK, MOBA_BLOCK)
        return (jnp.einsum('bhqj,bhjd->bhqd', p_own, v_own)
                + jnp.einsum('bhqsj,bhqsjd->bhqd', p_sel, v_sel))

    out = lax.map(one_query_block, (jnp.arange(nqb), q_sweep))
    out = jnp.moveaxis(out, 0, 2).reshape(B, H, s_pad, D)
    return out[:, :, :S]


def gated_delta_rule_chunked(q, k, v, g, beta):
    B, H, S, DK = q.shape
    DV = v.shape[-1]
    C = DN_CHUNK
    n = S // C
    q = (q * DK ** -0.5).reshape(B, H, n, C, DK)
    k = k.reshape(B, H, n, C, DK)
    v = v.reshape(B, H, n, C, DV)
    beta = beta.reshape(B, H, n, C)
    g = jnp.cumsum(g.reshape(B, H, n, C), axis=-1)
    incl = jnp.tril(jnp.ones((C, C), dtype=bool))
    strict = jnp.tril(jnp.ones((C, C), dtype=bool), k=-1)
    diff = g[..., :, None] - g[..., None, :]
    decay = jnp.where(incl, jnp.exp(jnp.where(incl, diff, 0.0)), 0.0)
    kb = k * beta[..., None]
    lmat = jnp.where(strict, jnp.einsum('bhnid,bhnjd->bhnij', kb, k) * decay, 0.0)
    amat = lmat + jnp.eye(C, dtype=jnp.float32)
    u = lax.linalg.triangular_solve(amat, v * beta[..., None], left_side=True, lower=True, unit_diagonal=True)
    w = lax.linalg.triangular_solve(amat, kb * jnp.exp(g)[..., None], left_side=True, lower=True, unit_diagonal=True)
    a_intra = jnp.einsum('bhnid,bhnjd->bhnij', q, k) * decay
    q_dec = q * jnp.exp(g)[..., None]
    k_dec = k * jnp.exp(g[..., -1:] - g)[..., None]
    g_last = jnp.exp(g[..., -1])

    def step(state, xs):
        q_i, a_i, w_i, u_i, k_i, gl_i = xs
        v_new = u_i - jnp.einsum('bhcd,bhde->bhce', w_i, state)
        o_i = jnp.einsum('bhcd,bhde->bhce', q_i, state) + jnp.einsum('bhcj,bhje->bhce', a_i, v_new)
        state = state * gl_i[..., None, None] + jnp.einsum('bhcd,bhce->bhde', k_i, v_new)
        return state, o_i

    xs = (jnp.moveaxis(q_dec, 2, 0), jnp.moveaxis(a_intra, 2, 0), jnp.moveaxis(w, 2, 0),
          jnp.moveaxis(u, 2, 0), jnp.moveaxis(k_dec, 2, 0), jnp.moveaxis(g_last, 2, 0))
    _, o = lax.scan(step, jnp.zeros((B, H, DK, DV), jnp.float32), xs)
    return jnp.moveaxis(o, 0, 2).reshape(B, H, S, DV)


def setup_inputs(seed: int = 0) -> dict:
    key = jax.random.key(seed)
    ks = jax.random.split(key, 12)
    f32 = jnp.float32
    x = jax.random.normal(ks[0], (BATCH, SEQ, D_MODEL), f32)
    rel_bias = 0.5 * jax.random.normal(ks[1], (REL_BUCKETS, ATTN_HEADS), f32)
    norm_w = 1.0 + 0.02 * jax.random.normal(ks[2], (DEPTH, D_MODEL), f32)
    w_in = jax.random.normal(ks[3], (DEPTH, D_MODEL, IN_COLS), f32) * D_MODEL ** -0.5
    q_norm_w = 1.0 + 0.02 * jax.random.normal(ks[4], (DEPTH, ATTN_HEAD_DIM), f32)
    k_norm_w = 1.0 + 0.02 * jax.random.normal(ks[5], (DEPTH, ATTN_HEAD_DIM), f32)
    conv_w = jax.random.normal(ks[6], (DEPTH, DN_CONV_WIDTH, DN_CONV_CH), f32) * DN_CONV_WIDTH ** -0.5
    a_log = jnp.log(jax.random.uniform(ks[7], (DEPTH, DN_HEADS), f32, minval=1.0, maxval=16.0))
    dt = jnp.exp(jax.random.uniform(ks[8], (DEPTH, DN_HEADS), f32,
                                    minval=math.log(1e-3), maxval=math.log(1e-1)))
    dt_bias = dt + jnp.log(-jnp.expm1(-dt))
    dn_norm_w = 1.0 + 0.02 * jax.random.normal(ks[9], (DEPTH, DN_HEAD_V), f32)
    w_out = jax.random.normal(ks[10], (DEPTH, D_MIX, D_MODEL), f32) * D_MIX ** -0.5
    return {"x": x, "rel_bias": rel_bias, "norm_w": norm_w, "w_in": w_in,
            "q_norm_w": q_norm_w, "k_norm_w": k_norm_w, "conv_w": conv_w,
            "a_log": a_log, "dt_bias": dt_bias, "dn_norm_w": dn_norm_w, "w_out": w_out}


def reference(x, rel_bias, norm_w, w_in, q_norm_w, k_norm_w, conv_w, a_log, dt_bias, dn_norm_w, w_out):
    B, S, _ = x.shape
    splits = [ATTN_WIDTH, 2 * ATTN_WIDTH, 3 * ATTN_WIDTH, 4 * ATTN_WIDTH,
              4 * ATTN_WIDTH + DN_CONV_CH, 4 * ATTN_WIDTH + DN_CONV_CH + DN_V_WIDTH,
              4 * ATTN_WIDTH + DN_CONV_CH + DN_V_WIDTH + DN_HEADS]
    for layer in range(DEPTH):
        h = rms_norm(x, norm_w[layer])
        proj = jnp.einsum('bsd,dc->bsc', h, w_in[layer])
        q_a, k_a, v_a, z_a, qkv_dn, z_dn, b_dn, a_dn = jnp.split(proj, splits, axis=-1)

        heads_a = lambda t: t.reshape(B, S, ATTN_HEADS, ATTN_HEAD_DIM).transpose(0, 2, 1, 3)
        qa = rms_norm(heads_a(q_a), q_norm_w[layer])
        ka = rms_norm(heads_a(k_a), k_norm_w[layer])
        va = heads_a(v_a)
        o_a = moba_attention(qa, ka, va, rel_bias)
        o_a = o_a.transpose(0, 2, 1, 3).reshape(B, S, ATTN_WIDTH)
        y_a = o_a * jax.nn.silu(z_a)

        qkv_dn = jax.nn.silu(causal_depthwise_conv(qkv_dn, conv_w[layer]))
        q_d, k_d, v_d = jnp.split(qkv_dn, [DN_QK_WIDTH, 2 * DN_QK_WIDTH], axis=-1)
        q_d = l2_norm(q_d.reshape(B, S, DN_HEADS, DN_HEAD_K)).transpose(0, 2, 1, 3)
        k_d = l2_norm(k_d.reshape(B, S, DN_HEADS, DN_HEAD_K)).transpose(0, 2, 1, 3)
        v_d = v_d.reshape(B, S, DN_HEADS, DN_HEAD_V).transpose(0, 2, 1, 3).astype(jnp.float32)
        beta = jax.nn.sigmoid(b_dn.astype(jnp.float32)).transpose(0, 2, 1)
        g = (-jnp.exp(a_log[layer].astype(jnp.float32))
             * jax.nn.softplus(a_dn.astype(jnp.float32) + dt_bias[layer].astype(jnp.float32))).transpose(0, 2, 1)
        o_d = gated_delta_rule_chunked(q_d, k_d, v_d, g, beta)
        o_d = o_d.transpose(0, 2, 1, 3).astype(x.dtype)
        o_d = rms_norm(o_d, dn_norm_w[layer]).reshape(B, S, DN_V_WIDTH)
        y_d = o_d * jax.nn.silu(z_dn)

        y = jnp.concatenate([y_a, y_d], axis=-1)
        x = x + jnp.einsum('bsc,cd->bsd', y, w_out[layer])
    return x
```

```python
import math
from contextlib import ExitStack

import numpy as np
import ml_dtypes
import concourse.bass as bass
import concourse.mybir as mybir
from concourse.bass_utils import run_bass_kernel_spmd

F32 = mybir.dt.float32
BF16 = mybir.dt.bfloat16
AF = mybir.ActivationFunctionType
ALU = mybir.AluOpType
AX = mybir.AxisListType

NCORES = 8
S_TOT = 8192
D = 1024
SUP = 512
NCOL = 2056
QA0, KA0, VA0, ZA0 = 0, 256, 512, 768
QD0, KD0, VD0, ZD0, BA0, GCOL = 0, 256, 512, 768, 1024, 1028
EPS = 1e-6
DBG_ONE = False
SEQ_DEBUG = False
NEG = -30000.0


class SemObj:
    def __init__(self, h, scale):
        self.h = h
        self.scale = scale
        self.count = 0


class Buf:
    def __init__(self, name):
        self.name = name
        self.w = None
        self.r = {}


class Opnd:
    def __init__(self, ap, buf):
        self.ap = ap
        self.buf = buf


class Tile:
    def __init__(self, t, name, excl=False):
        self.t = t
        self.buf = Buf(name)
        self.excl = excl

    def __getitem__(self, idx):
        return Opnd(self.t[idx], self.buf)

    def v(self, idx, buf):
        return Opnd(self.t[idx], buf)


class PEProxy:
    def __init__(self, eng):
        self._eng = eng

    def matmul(self, *a, **k):
        k.setdefault("skip_group_check", True)
        return self._eng.matmul(*a, **k)

    def __getattr__(self, name):
        return getattr(self._eng, name)


class Sched:
    def __init__(self, nc, stack):
        self.nc = nc
        self.stack = stack
        self.engs = {"pe": PEProxy(nc.tensor), "act": nc.scalar, "dve": nc.vector, "pool": nc.gpsimd, "sp": nc.sync}
        self.sem = {k: SemObj(stack.enter_context(nc.semaphore("s_" + k)), 1) for k in self.engs}
        self.seen = {k: {} for k in self.engs}
        self.psum_bufs = set()
        self.n_ops = 0
        self.all_sems = list(self.sem.values())

    def dma_sem(self, name):
        so = SemObj(self.stack.enter_context(self.nc.semaphore("d_" + name)), 16)
        self.all_sems.append(so)
        return so

    def barrier(self):
        for e, eng in self.engs.items():
            seen = self.seen[e]
            for so in self.all_sems:
                if so.count > seen.get(so, 0) and not (so is self.sem[e]):
                    eng.wait_ge(so.h, so.count * so.scale)
                    seen[so] = so.count

    def op(self, e, fn, reads=(), writes=(), dsem=None):
        deps = {}
        own = self.sem[e]

        def add(so, c):
            if so is own and e == "pe" and dsem is None:
                return
            if deps.get(so, 0) < c:
                deps[so] = c

        rb, wb = [], []
        for o in reads:
            b = o.buf if isinstance(o, Opnd) else o
            (wb if b in self.psum_bufs else rb).append(b)
        for o in writes:
            wb.append(o.buf if isinstance(o, Opnd) else o)
        for b in rb:
            if b.w:
                add(*b.w)
        for b in wb:
            if b.w:
                add(*b.w)
            for so, c in b.r.items():
                add(so, c)
        seen = self.seen[e]
        eng = self.engs[e]
        for so, c in deps.items():
            if seen.get(so, 0) < c:
                eng.wait_ge(so.h, c * so.scale)
                seen[so] = c
        ins = fn(eng)
        so = dsem if dsem is not None else own
        so.count += 1
        ins.then_inc(so.h, so.scale)
        for b in rb:
            if b.r.get(so, 0) < so.count:
                b.r[so] = so.count
        for b in wb:
            b.w = (so, so.count)
            b.r = {}
        self.n_ops += 1
        return ins


def t5_bucket_np(n):
    if n < 16:
        return n
    v = np.float32(np.log(np.float32(n) / np.float32(16.0))) / np.float32(math.log(8.0)) * np.float32(16.0)
    return min(31, 16 + int(v))


def host_constants():
    bf = ml_dtypes.bfloat16
    ident = np.eye(128, dtype=np.float32)
    jflip = ident[::-1].copy()
    m = np.arange(128)[:, None]
    i = np.arange(128)[None, :]
    triu = (m <= i).astype(np.float32)
    omt = 1.0 - triu
    mbs = np.where(i >= m, NEG, 0.0).astype(np.float32)
    mbi = np.where(m > i, NEG, 0.0).astype(np.float32)
    cf32 = np.concatenate([jflip, triu, omt, ident], axis=1).astype(np.float32)
    m32 = (m // 32 == i // 32).astype(np.float32)
    m64 = (m // 64 == i // 64).astype(np.float32)
    cbf = np.concatenate([ident, mbs, mbi, m32, m64 - m32, 1.0 - m64], axis=1).astype(bf)
    onehot = np.zeros((32, S_TOT), dtype=np.float32)
    onehot[np.arange(S_TOT) // 256, np.arange(S_TOT)] = 1.0
    sel = np.zeros((33, 384), dtype=np.float32)
    for idx in range(383):
        dist = idx - 127
        if dist < 0:
            sel[32, idx] = NEG
        else:
            sel[t5_bucket_np(dist), idx] += 1.0
            sel[31, idx] -= 1.0
    return cf32, cbf, onehot.astype(bf), sel


def build_program(n_super=16, dbg=None, do_out=True, passes=("A0", "A1", "D")):
    dbg = dbg or []
    S_LOC = n_super * SUP
    nc = bass.Bass("TRN2", target_bir_lowering=False)
    x_d = nc.dram_tensor("x", [S_TOT, D], F32, kind="ExternalInput").ap()
    win_d = nc.dram_tensor("w_in", [3, D, NCOL], F32, kind="ExternalInput").ap()
    wout_d = nc.dram_tensor("w_out", [D, D], F32, kind="ExternalInput").ap()
    vecs_d = nc.dram_tensor("vecs", [3, 128, 72], F32, kind="ExternalInput").ap()
    relb_d = nc.dram_tensor("relb", [2, 32, 4], F32, kind="ExternalInput").ap()
    cf32_d = nc.dram_tensor("cf32", [128, 512], F32, kind="ExternalInput").ap()
    cbf_d = nc.dram_tensor("cbf", [128, 768], BF16, kind="ExternalInput").ap()
    oneh_d = nc.dram_tensor("onehot", [32, S_TOT], BF16, kind="ExternalInput").ap()
    sel_d = nc.dram_tensor("sel", [33, 384], F32, kind="ExternalInput").ap()
    out_d = nc.dram_tensor("out", [S_TOT, D], F32, kind="ExternalOutput").ap()
    fscr_d = nc.dram_tensor("fscr", [4, 384], F32, kind="Internal").ap()
    ys_d = nc.dram_tensor("ys", [1024, S_LOC], BF16, kind="Internal").ap()
    hts_d = nc.dram_tensor("hts", [1024, S_LOC], BF16, kind="Internal").ap()
    dbg_d = {}
    for name, shape, dt in dbg:
        dbg_d[name] = nc.dram_tensor("dbg_" + name, shape, dt, kind="ExternalOutput").ap()

    with ExitStack() as st:
        S = Sched(nc, st)

        def sb(name, shape, dt, stack=None):
            return Tile((stack or st).enter_context(nc.sbuf_tensor("sb_" + name, shape, dt)), name)

        def ps(name, shape, dt):
            t = Tile(st.enter_context(nc.psum_tensor("ps_" + name, shape, dt)), name, excl=True)
            S.psum_bufs.add(t.buf)
            return t

        pT = ps("pT", [128, 1024], BF16)
        pA = [ps("pA0", [128, 512], F32), ps("pA1", [128, 512], F32)]
        pS = [ps("pS0", [128, 512], F32), ps("pS1", [128, 512], F32)]
        pO = ps("pO", [128, 512], F32)
        pD = [ps("pD0", [128, 512], F32), ps("pD1", [128, 512], F32)]

        vecs = sb("vecs", [128, 72], F32)
        cf32 = sb("cf32", [128, 512], F32)
        cbf = sb("cbf", [128, 768], BF16)
        onesf = sb("onesf", [128, 128], F32)
        negonesf = sb("negonesf", [128, 128], F32)
        onesb = sb("onesb", [128, 128], BF16)
        Wb = sb("Wb", [128, 8, NCOL], BF16)
        xt = [sb("xt0", [128, 1024], F32), sb("xt1", [128, 1024], F32)]
        d_xt = [S.dma_sem("xt0"), S.dma_sem("xt1")]
        xb = sb("xb", [128, 1024], BF16)
        junk = sb("junk", [128, 1024], BF16)
        ssq = sb("ssq", [128, 2], F32)
        hT = sb("hT", [128, 8, SUP], BF16)
        sqt = [sb("sqt0", [128, SUP], BF16), sb("sqt1", [128, SUP], BF16)]
        qraw = sb("qraw", [128, SUP], F32)
        rinv = sb("rinv", [128, SUP], F32)

        JF = cf32[:, 0:128]
        TRIU = cf32[:, 128:256]
        OMT = cf32[:, 256:384]
        IDF = cf32[:, 384:512]
        IDB = cbf[:, 0:128]
        MBS = cbf[:, 128:256]
        MBI = cbf[:, 256:384]
        M32 = cbf[:, 384:512]
        MB1 = cbf[:, 512:640]
        MB2 = cbf[:, 640:768]

        def dma(out, in_, sem, reads=(), writes=(), eng="sp"):
            return S.op(eng, lambda e: e.dma_start(out=out, in_=in_), reads=reads, writes=writes, dsem=sem)

        d_const = S.dma_sem("const")
        d_vecs = S.dma_sem("vecs")
        d_t5 = S.dma_sem("t5")
        d_t5b = S.dma_sem("t5b")
        d_fs = S.dma_sem("fs")
        d_hk = S.dma_sem("hk")
        d_dbg = S.dma_sem("dbg")
        d_ys = S.dma_sem("ys")
        fscr = Buf("fscr")
        ysb = Buf("ys")
        d_hts = [S.dma_sem("hts0"), S.dma_sem("hts1")]
        d_hld = [S.dma_sem("hld0"), S.dma_sem("hld1")]
        htsb = [Buf("hts_%d" % i) for i in range(n_super)]
        hcache = {"have": False}

        def hT_store(s, hTd, slot):
            t0 = s * SUP
            dma(hts_d[:, t0:t0 + SUP].rearrange("(c p) t -> p c t", p=128), hTd.t[:, :, :], d_hts[slot], reads=[hTd.buf], writes=[htsb[s]])

        def hT_load(s, hTd, slot):
            t0 = s * SUP
            dma(hTd.t[:, :, :], hts_d[:, t0:t0 + SUP].rearrange("(c p) t -> p c t", p=128), d_hld[slot], reads=[htsb[s]], writes=[hTd.buf])

        dma(cf32.t[:], cf32_d[:, :], d_const, writes=[cf32.buf])
        dma(cbf.t[:], cbf_d[:, :], S.dma_sem("cbf"), writes=[cbf.buf])
        S.op("dve", lambda e: e.memset(onesf.t[:], 1.0), writes=[onesf.buf])
        S.op("dve", lambda e: e.memset(negonesf.t[:], -1.0), writes=[negonesf.buf])
        S.op("dve", lambda e: e.memset(onesb.t[:], 1.0), writes=[onesb.buf])

        def dump(name, src_ap, bufs, dst_idx=None):
            if name in dbg_d:
                dst = dbg_d[name] if dst_idx is None else dbg_d[name][dst_idx]
                dma(dst, src_ap, d_dbg, reads=bufs, eng="pool")

        pa_i = [0]

        def next_pA():
            pa_i[0] += 1
            return pA[pa_i[0] % 2]

        xk = [0]
        ev_i = [0]

        def evac_eng():
            ev_i[0] += 1
            return "act" if ev_i[0] % 2 == 0 else "dve"

        def load_weights(pi, ncols):
            dma(vecs.t[:], vecs_d[pi, :, :], d_vecs, writes=[vecs.buf])
            k = 0
            for c in range(8):
                for c0 in range(0, ncols, 1024):
                    c1 = min(ncols, c0 + 1024)
                    t = xt[k % 2]
                    dma(t.t[:, 0:c1 - c0], win_d[pi, c * 128:(c + 1) * 128, c0:c1], d_xt[k % 2], writes=[t.buf])
                    S.op("dve",
                         lambda e: e.tensor_scalar(out=Wb.t[:, c, c0:c1], in0=t.t[:, 0:c1 - c0], scalar1=vecs.t[:, c:c + 1],
                                                   scalar2=None, op0=ALU.mult), reads=[t.buf, vecs.buf], writes=[Wb.buf])
                    k += 1

        hTcur = [hT]

        def x_to_hT_gen(s, hTd=None, sq_eng="act"):
            hTd = hTd or hT
            t0 = s * SUP
            for tt in range(4):
                r0 = t0 + tt * 128
                xi = xk[0] % 2
                xk[0] += 1
                X = xt[xi]
                dma(X.t[:], x_d[r0:r0 + 128, :], d_xt[xi], writes=[X.buf])
                if sq_eng == "dve":
                    S.op("dve", lambda e: e.scalar_tensor_tensor(out=junk.t[:], in0=X.t[:], scalar=1.0, in1=X.t[:], op0=ALU.mult, op1=ALU.mult,
                                                                 accum_out=ssq.t[:, 0:1]),
                         reads=[X.buf], writes=[junk.buf, ssq.buf])
                else:
                    S.op("act", lambda e: e.activation(out=junk.t[:], in_=X.t[:], func=AF.Square, accum_out=ssq.t[:, 0:1]),
                         reads=[X.buf], writes=[junk.buf, ssq.buf])
                S.op("act", lambda e: e.activation(out=ssq.t[:, 1:2], in_=ssq.t[:, 0:1], func=AF.Ln, scale=1.0 / D, bias=EPS),
                     reads=[ssq.buf], writes=[ssq.buf])
                S.op("act", lambda e: e.activation(out=ssq.t[:, 1:2], in_=ssq.t[:, 1:2], func=AF.Exp, scale=-0.5),
                     reads=[ssq.buf], writes=[ssq.buf])
                S.op("dve", lambda e: e.tensor_scalar(out=xb.t[:], in0=X.t[:], scalar1=ssq.t[:, 1:2], scalar2=None,
                                                      op0=ALU.mult), reads=[X.buf, ssq.buf], writes=[xb.buf])
                for c in range(8):
                    S.op("pe", lambda e: e.transpose(out=pT.t[:, c * 128:(c + 1) * 128], in_=xb.t[:, c * 128:(c + 1) * 128],
                                                     identity=IDB.ap), reads=[xb.buf, cbf.buf], writes=[pT.buf])
                S.op("dve", lambda e: e.tensor_copy(out=hTd.t[:, 0:4, tt * 128:(tt + 1) * 128], in_=hTd.t[:, 0:4, tt * 128:(tt + 1) * 128]),
                     reads=[], writes=[]) if False else None
                S.op("act", lambda e: e.copy(out=hTd.t[:, :, tt * 128:(tt + 1) * 128],
                                             in_=pT.t[:, :].rearrange("p (c t) -> p c t", c=8)),
                     reads=[pT.buf], writes=[hTd.buf])
                yield

        def x_to_hT(s):
            for _ in x_to_hT_gen(s):
                pass

        def proj_fm(col0, m, P, hTs=None):
            hTs = hTs or hT
            for c in range(8):
                S.op("pe", lambda e: e.matmul(P.t[0:m, :], lhsT=Wb.t[:, c, col0:col0 + m], rhs=hTs.t[:, c, :],
                                              start=(c == 0), stop=(c == 7)),
                     reads=[Wb.buf, hTs.buf], writes=[P.buf])

        def proj_fm_gen(col0, m, P, hTs, step=8):
            for c in range(8):
                S.op("pe", lambda e: e.matmul(P.t[0:m, :], lhsT=Wb.t[:, c, col0:col0 + m], rhs=hTs.t[:, c, :],
                                              start=(c == 0), stop=(c == 7)),
                     reads=[Wb.buf, hTs.buf], writes=[P.buf])
                if c % step == step - 1:
                    yield

        def run_rr(gens):
            active = list(gens)
            while active:
                for gen in list(active):
                    try:
                        next(gen)
                    except StopIteration:
                        active.remove(gen)

        with ExitStack() as stA:
            sa = lambda name, shape, dt: sb(name, shape, dt, stack=stA)
            KA = [sa("KA%d" % h, [96, S_LOC], BF16) for h in range(4)]
            KAb = [[Buf("KA%d_%d" % (h, s)) for s in range(n_super)] for h in range(4)]
            KAone = Buf("KAone")
            VA = sa("VA", [128, S_LOC // 128, 4, 65], BF16)
            VAb = [Buf("VA_%d" % s) for s in range(n_super)]
            KS = [sa("KS%d" % h, [64, 32], BF16) for h in range(4)]
            KSf = sa("KSf", [64, 2], F32)
            KSf2 = [KSf, sa("KSf_b", [64, 2], F32)]
            NBd = [sa("NBd%d" % h, [128, 128], BF16) for h in range(4)]
            NBo = [sa("NBo%d" % h, [128, 128], BF16) for h in range(4)]
            wq = sa("wq", [64, 2], F32)
            qraw2 = [qraw, sa("qraw_b", [64, SUP], F32)]
            rinv2 = [rinv, sa("rinv_b", [64, SUP], F32)]
            QA2 = [[sa("QA%d_%d" % (h, i), [96, SUP], BF16) for h in range(4)] for i in range(2)]
            SZA2 = [sa("SZA_%d" % i, [64, 4, SUP], BF16) for i in range(2)]
            hT2 = [hT, sa("hT_b", [128, 8, SUP], BF16)]
            SBK = [pS[0], pS[1], pD[0], pD[1]]
            PT = [sa("PT%d" % i, [128, SUP], BF16) for i in range(4)]
            OS = sa("OS", [65, SUP], F32)
            RD = sa("RD", [65, SUP], F32)
            RDh = sa("RDh", [65, 2 * SUP], BF16)
            G = [sa("G%d" % i, [128, 4, 32], F32) for i in range(2)]
            M8 = sa("M8", [128, 4, 8], F32)
            MBQ = [sa("MBQ%d" % i, [128, 96], BF16) for i in range(4)]
            YA = sa("YA", [64, 4, SUP], BF16)
            relb = sa("relb", [33, 4], F32)
            selt = Tile(qraw2[1].t, "selt_alias"); selt.buf = qraw2[1].buf
            fsb = Tile(rinv2[1].t, "fsb_alias"); fsb.buf = rinv2[1].buf
            hank = Tile(rinv.t, "hank_alias"); hank.buf = rinv.buf
            d_oh = S.dma_sem("oh")
            for h in range(4):
                dma(KA[h].t[64:96, :], oneh_d[:, 0:S_LOC], d_oh, writes=[KAone])
            S.op("dve", lambda e: e.memset(VA.t[:], 1.0), writes=VAb)

            def attn_pass(hh):
                use_cache = hcache["have"]
                hcache["have"] = True
                load_weights(hh, 1024)
                S.op("dve", lambda e: e.tensor_copy(out=wq.t[:, 0:1], in_=vecs.t[0:64, 8:9]), reads=[vecs.buf], writes=[wq.buf])
                S.op("dve", lambda e: e.tensor_scalar(out=wq.t[:, 1:2], in0=vecs.t[0:64, 9:10], scalar1=8.0, scalar2=None,
                                                      op0=ALU.mult), reads=[vecs.buf], writes=[wq.buf])
                S.op("dve", lambda e: e.memset(relb.t[32:33, :], 1.0), writes=[relb.buf])
                dma(relb.t[0:32, :], relb_d[hh, :, :], d_t5, writes=[relb.buf])
                dma(selt.t[0:33, 0:384], sel_d[:, :], d_t5b, writes=[selt.buf])
                S.op("pe", lambda e: e.matmul(pA[0].t[0:4, 0:384], lhsT=relb.t[:, :], rhs=selt.t[0:33, 0:384], start=True, stop=True),
                     reads=[relb.buf, selt.buf], writes=[pA[0].buf])
                S.op("dve", lambda e: e.tensor_copy(out=fsb.t[0:4, 0:384], in_=pA[0].t[0:4, 0:384]), reads=[pA[0].buf], writes=[fsb.buf])
                dma(fscr_d[:, :], fsb.t[0:4, 0:384], d_fs, reads=[fsb.buf], writes=[fscr])
                for h in range(4):
                    for var, off, dst in ((0, 0, NBd), (1, 128, NBo)):
                        src = bass.AP(tensor=fscr_d.tensor, offset=h * 384 + off, ap=[[1, 128], [1, 128]])
                        dma(hank.t[:, 0:128], src, d_hk, reads=[fscr], writes=[hank.buf])
                        S.op("pe", lambda e: e.matmul(pA[1].t[:, 0:128], lhsT=JF.ap, rhs=hank.t[:, 0:128], start=True, stop=True),
                             reads=[cf32.buf, hank.buf], writes=[pA[1].buf])
                        S.op("dve", lambda e: e.tensor_copy(out=dst[h].t[:], in_=pA[1].t[:, 0:128]),
                             reads=[pA[1].buf], writes=[dst[h].buf])


                for h in range(4):
                    S.op("dve", lambda e: e.memset(KS[h].t[:], 0.0), writes=[KS[h].buf])
                    for i in range(2):
                        S.op("dve", lambda e: e.memset(QA2[i][h].t[64:96, :], 0.0), writes=[QA2[i][h].buf])
                for i in range(2):
                    S.op("dve", lambda e: e.memset(G[i].t[:], -1e30), writes=[G[i].buf])
                for i in range(4):
                    S.op("dve", lambda e: e.memset(MBQ[i].t[:], 0.0), writes=[MBQ[i].buf])

                def prod(s):
                    t0 = s * SUP
                    hTd, QAs, SZAs = hT2[s % 2], QA2[s % 2], SZA2[s % 2]
                    if use_cache:
                        hT_load(s, hTd, s % 2)
                        yield
                    for tt in (range(4) if not use_cache else ()):
                        r0 = t0 + tt * 128
                        xi = xk[0] % 2
                        xk[0] += 1
                        X = xt[xi]
                        dma(X.t[:], x_d[r0:r0 + 128, :], d_xt[xi], writes=[X.buf])
                        yield
                        S.op("dve", lambda e: e.scalar_tensor_tensor(out=junk.t[:], in0=X.t[:], scalar=1.0, in1=X.t[:], op0=ALU.mult, op1=ALU.mult,
                                                                     accum_out=ssq.t[:, 0:1]),
                             reads=[X.buf], writes=[junk.buf, ssq.buf])
                        yield
                        S.op("act", lambda e: e.activation(out=ssq.t[:, 1:2], in_=ssq.t[:, 0:1], func=AF.Ln, scale=1.0 / D, bias=EPS),
                             reads=[ssq.buf], writes=[ssq.buf])
                        yield
                        S.op("act", lambda e: e.activation(out=ssq.t[:, 1:2], in_=ssq.t[:, 1:2], func=AF.Exp, scale=-0.5),
                             reads=[ssq.buf], writes=[ssq.buf])
                        yield
                        S.op("dve", lambda e: e.tensor_scalar(out=xb.t[:], in0=X.t[:], scalar1=ssq.t[:, 1:2], scalar2=None,
                                                              op0=ALU.mult), reads=[X.buf, ssq.buf], writes=[xb.buf])
                        yield
                        for c in range(8):
                            S.op("pe", lambda e: e.transpose(out=pT.t[:, c * 128:(c + 1) * 128], in_=xb.t[:, c * 128:(c + 1) * 128],
                                                             identity=IDB.ap), reads=[xb.buf, cbf.buf], writes=[pT.buf])
                            if c == 3:
                                yield
                        yield
                        S.op("dve", lambda e: e.tensor_copy(out=hTd.t[:, :, tt * 128:(tt + 1) * 128],
                                                            in_=pT.t[:, :].rearrange("p (c t) -> p c t", c=8)),
                             reads=[pT.buf], writes=[hTd.buf])
                        yield
                    if not use_cache:
                        hT_store(s, hTd, s % 2)
                    if s == 0 and hh == 0:
                        dump("hT", hTd.t[:, :, :], [hTd.buf])
                    def qk_chain(ci_):
                        P = pA[ci_]
                        sq, qr, ri = sqt[ci_], qraw2[ci_], rinv2[ci_]
                        for gi in range(ci_, 8, 2):
                            which, h = gi // 4, gi % 4
                            yield from proj_fm_gen((QA0 if which == 0 else KA0) + h * 64, 64, P, hTd)
                            S.op("act", lambda e: e.activation(out=sq.t[0:64, :], in_=P.t[0:64, :], func=AF.Square),
                                 reads=[P.buf], writes=[sq.buf])
                            yield
                            S.op("dve", lambda e: e.tensor_copy(out=qr.t[0:64, :], in_=P.t[0:64, :]), reads=[P.buf], writes=[qr.buf])
                            S.op("pe", lambda e: e.matmul(P.t[0:64, :], lhsT=onesb.t[0:64, 0:64], rhs=sq.t[0:64, :], start=True, stop=True),
                                 reads=[onesb.buf, sq.buf], writes=[P.buf])
                            yield
                            S.op("act", lambda e: e.activation(out=ri.t[0:64, :], in_=P.t[0:64, :], func=AF.Ln, bias=64.0 * EPS),
                                 reads=[P.buf], writes=[ri.buf])
                            yield
                            S.op("act", lambda e: e.activation(out=ri.t[0:64, :], in_=ri.t[0:64, :], func=AF.Exp, scale=-0.5),
                                 reads=[ri.buf], writes=[ri.buf])
                            yield
                            if which == 0:
                                dst = QAs[h][0:64, :]
                            else:
                                dst = KA[h].v((slice(0, 64), slice(t0, t0 + SUP)), KAb[h][s])
                            S.op("dve", lambda e: e.scalar_tensor_tensor(out=dst.ap, in0=qr.t[0:64, :], scalar=wq.t[:, which:which + 1],
                                                                         in1=ri.t[0:64, :], op0=ALU.mult, op1=ALU.mult),
                                 reads=[qr.buf, wq.buf, ri.buf], writes=[dst.buf])
                            if which == 1:
                                yield
                                ksf = KSf2[ci_]
                                S.op("dve", lambda e: e.tensor_reduce(out=ksf.t[:, 0:2],
                                                                      in_=KA[h].t[0:64, t0:t0 + SUP].rearrange("p (b k) -> p b k", b=2),
                                                                      axis=AX.X, op=ALU.add),
                                     reads=[KAb[h][s]], writes=[ksf.buf])
                                yield
                                S.op("dve", lambda e: e.tensor_copy(out=KS[h].t[:, 2 * s:2 * s + 2], in_=ksf.t[:, 0:2]),
                                     reads=[ksf.buf], writes=[KS[h].buf])
                                yield

                    qk_active = [qk_chain(0), qk_chain(1)]
                    while qk_active:
                        for gen_ in list(qk_active):
                            try:
                                next(gen_)
                                yield
                            except StopIteration:
                                qk_active.remove(gen_)
                    if s == 0 and hh == 0:
                        for h in range(4):
                            dump("qa", QAs[h].t[0:64, :], [QAs[h].buf], (h,))
                            dump("ka", KA[h].t[0:64, 0:SUP], [KAb[h][0]], (h,))
                    for tt in range(4):
                        P = next_pA()
                        for c in range(8):
                            S.op("pe", lambda e: e.matmul(P.t[:, 0:256], lhsT=hTd.t[:, c, tt * 128:(tt + 1) * 128], rhs=Wb.t[:, c, VA0:VA0 + 256],
                                                          start=(c == 0), stop=(c == 7)), reads=[Wb.buf, hTd.buf], writes=[P.buf])
                            if c % 4 == 3:
                                yield
                        S.op("dve", lambda e: e.tensor_copy(out=VA.t[:, 4 * s + tt, :, 0:64], in_=P.t[:, 0:256].rearrange("p (h d) -> p h d", h=4)),
                             reads=[P.buf], writes=[VAb[s]])
                        yield
                    for h in range(4):
                        P = next_pA()
                        zt = rinv2[h % 2]
                        yield from proj_fm_gen(ZA0 + h * 64, 64, P, hTd)
                        S.op("act", lambda e: e.activation(out=zt.t[0:64, :], in_=P.t[0:64, :], func=AF.Exp, scale=-1.0), reads=[P.buf], writes=[zt.buf])
                        yield
                        S.op("act", lambda e: e.activation(out=zt.t[0:64, :], in_=zt.t[0:64, :], func=AF.Ln, bias=1.0), reads=[zt.buf], writes=[zt.buf])
                        yield
                        S.op("act", lambda e: e.activation(out=zt.t[0:64, :], in_=zt.t[0:64, :], func=AF.Exp, scale=-1.0), reads=[zt.buf], writes=[zt.buf])
                        yield
                        S.op("dve", lambda e: e.tensor_tensor(out=SZAs.t[:, h, :], in0=P.t[0:64, :], in1=zt.t[0:64, :], op=ALU.mult),
                             reads=[P.buf, zt.buf], writes=[SZAs.buf])
                        yield
                    for tt in range(4):
                        T = 4 * s + tt
                        ob = T // 2
                        if ob == 0:
                            continue
                        Gt = G[T % 2]
                        P = next_pA()
                        for h in range(4):
                            S.op("pe", lambda e: e.matmul(P.t[:, h * 32:h * 32 + ob], lhsT=QAs[h].t[0:64, tt * 128:(tt + 1) * 128],
                                                          rhs=KS[h].t[:, 0:ob], start=(h == 0), stop=(h == 3)),
                                 reads=[QAs[h].buf, KS[h].buf], writes=[P.buf])
                        yield
                        S.op("dve", lambda e: e.tensor_copy(out=Gt.t[:, :, 0:ob], in_=P.t[:, 0:128].rearrange("p (h n) -> p h n", h=4)[:, :, 0:ob]),
                             reads=[P.buf], writes=[Gt.buf])
                        yield
                        for h in range(4):
                            S.op("dve", lambda e: e.max(out=M8.t[:, h, :], in_=Gt.t[:, h, :]), reads=[Gt.buf], writes=[M8.buf])
                        yield
                        for h in range(4):
                            mq = MBQ[h]
                            S.op("dve", lambda e: e.tensor_scalar(out=mq.t[:, 64:64 + ob], in0=Gt.t[:, h, 0:ob], scalar1=M8.t[:, h, 2:3],
                                                                  scalar2=NEG, op0=ALU.is_lt, op1=ALU.mult),
                                 reads=[Gt.buf, M8.buf], writes=[mq.buf])
                        yield
                        for h in range(4):
                            mq = MBQ[h]
                            P2 = pA[h % 2]
                            S.op("pe", lambda e: e.matmul(P2.t[0:96, 0:128], lhsT=mq.t[:, :], rhs=IDB.ap, start=True, stop=True),
                                 reads=[mq.buf, cbf.buf], writes=[P2.buf])
                            yield
                            S.op("dve", lambda e: e.tensor_copy(out=QAs[h].t[64:96, tt * 128:(tt + 1) * 128], in_=P2.t[64:96, 0:128]),
                                 reads=[P2.buf], writes=[QAs[h].buf])
                            yield
                    if s == 1 and hh == 0:
                        for h in range(4):
                            dump("qam", QAs[h].t[64:96, :], [QAs[h].buf], (h,))

                def cons(s):
                    t0 = s * SUP
                    QAs, SZAs = QA2[s % 2], SZA2[s % 2]
                    nkt = 4 * (s + 1)
                    units = [(h, kt) for h in range(4) for kt in range(nkt)]
                    uinfo = {}

                    def att_qk(ui):
                        h, kt = units[ui]
                        j = kt - 4 * s
                        c0 = 0 if j <= 0 else j * 128
                        P = SBK[ui % 4]
                        near = []
                        if j >= 0:
                            near.append((NBd[h], j * 128))
                            if j < 3:
                                near.append((NBo[h], (j + 1) * 128))
                        elif j == -1:
                            near.append((NBo[h], 0))
                        S.op("pe", lambda e: e.matmul(P.t[:, c0:SUP], lhsT=KA[h].t[0:96, kt * 128:(kt + 1) * 128], rhs=QAs[h].t[0:96, c0:SUP],
                                                      start=True, stop=(len(near) == 0)),
                             reads=[KAb[h][kt // 4], KAone, QAs[h].buf], writes=[P.buf])
                        for ni, (nb, qc) in enumerate(near):
                            S.op("pe", lambda e: e.matmul(P.t[:, qc:qc + 128], lhsT=IDB.ap, rhs=nb.t[:, :], start=False,
                                                          stop=(ni == len(near) - 1)),
                                 reads=[cbf.buf, nb.buf], writes=[P.buf])
                        uinfo[ui] = (P, c0)

                    def att_exp_pv(ui):
                        h, kt = units[ui]
                        P, c0 = uinfo.pop(ui)
                        pt = PT[ui % len(PT)]
                        S.op("act", lambda e: e.activation(out=pt.t[:, c0:SUP], in_=P.t[:, c0:SUP], func=AF.Exp),
                             reads=[P.buf], writes=[pt.buf])
                        S.op("pe", lambda e: e.matmul(pO.t[0:65, c0:SUP], lhsT=VA.t[:, kt, h, :], rhs=pt.t[:, c0:SUP],
                                                      start=(kt == 0), stop=(kt == nkt - 1)),
                             reads=[VAb[kt // 4], pt.buf], writes=[pO.buf])

                    def att_tail_a(h):
                        S.op("act", lambda e: e.copy(out=OS.t[:, :], in_=pO.t[0:65, :]), reads=[pO.buf], writes=[OS.buf])
                        if s == 1 and hh == 0 and h == 0:
                            dump("os", OS.t[:, :], [OS.buf])
                        S.op("act", lambda e: e.activation(out=RD.t[64:65, :], in_=OS.t[64:65, :], func=AF.Ln), reads=[OS.buf], writes=[RD.buf])
                        S.op("act", lambda e: e.activation(out=RD.t[64:65, :], in_=RD.t[64:65, :], func=AF.Exp, scale=-1.0), reads=[RD.buf], writes=[RD.buf])
                        S.op("dve", lambda e: e.tensor_copy(out=RDh.t[64:65, 0:SUP], in_=RD.t[64:65, :]), reads=[RD.buf], writes=[RDh.buf])
                        S.op("dve", lambda e: e.tensor_tensor(out=RDh.t[64:65, SUP:2 * SUP], in0=RD.t[64:65, :], in1=RDh.t[64:65, 0:SUP], op=ALU.subtract),
                             reads=[RD.buf, RDh.buf], writes=[RDh.buf])

                    def att_tail_b(h, ui):
                        P = SBK[ui % 4]
                        S.op("pe", lambda e: e.matmul(P.t[0:64, :], lhsT=onesb.t[64:65, 0:64], rhs=RDh.t[64:65, 0:SUP], start=True, stop=False),
                             reads=[onesb.buf, RDh.buf], writes=[P.buf])
                        S.op("pe", lambda e: e.matmul(P.t[0:64, :], lhsT=onesb.t[64:65, 0:64], rhs=RDh.t[64:65, SUP:2 * SUP], start=False, stop=True),
                             reads=[onesb.buf, RDh.buf], writes=[P.buf])
                        S.op("dve", lambda e: e.tensor_tensor(out=OS.t[0:64, :], in0=OS.t[0:64, :], in1=P.t[0:64, :], op=ALU.mult),
                             reads=[OS.buf, P.buf], writes=[OS.buf])
                        S.op("dve", lambda e: e.tensor_tensor(out=YA.t[:, h, :], in0=OS.t[0:64, :], in1=SZAs.t[:, h, :], op=ALU.mult),
                             reads=[OS.buf, SZAs.buf], writes=[YA.buf])
                        if "oa" in dbg_d and hh == 0:
                            dump("oa", OS.t[0:64, :], [OS.buf], (h, slice(None), slice(t0, t0 + SUP)))

                    LOOK = 3
                    for u0 in range(min(LOOK, len(units))):
                        att_qk(u0)
                    pend_tail = None
                    for ui in range(len(units)):
                        h, kt = units[ui]
                        if ui + LOOK < len(units):
                            att_qk(ui + LOOK)
                        att_exp_pv(ui)
                        if pend_tail is not None and kt == 1:
                            att_tail_b(pend_tail, ui)
                            pend_tail = None
                        if kt == nkt - 1:
                            att_tail_a(h)
                            pend_tail = h
                        yield
                    att_tail_b(pend_tail, len(units) - 1)
                    dma(ys_d[256 * hh:256 * hh + 256, t0:t0 + SUP].rearrange("(h p) t -> p h t", h=4), YA.t[:, :, :], d_ys, reads=[YA.buf], writes=[ysb], eng="sp")


                n_run = n_super if not DBG_ONE else 1
                if SEQ_DEBUG:
                    for s in range(n_run):
                        run_rr([prod(s)])
                        run_rr([cons(s)])
                else:
                    run_rr([prod(0)])
                    for s in range(n_run):
                        run_rr([cons(s)] + ([prod(s + 1)] if s + 1 < n_run else []))

            for pname in passes:
                if pname[0] == "A":
                    attn_pass(int(pname[1]))
            S.barrier()
            S.barrier()

        with ExitStack() as stD:
            sd = lambda name, shape, dt: sb(name, shape, dt, stack=stD)
            negA = sd("negA", [128, 4], F32)
            SZD2 = [sd("SZD_%d" % i, [128, 2, SUP], BF16) for i in range(2)]
            CIN = [sd("CIN0", [128, SUP + 3], F32), sd("CIN1", [128, SUP + 3], F32)]
            CARRY = sd("CARRY", [128, 12, 3], F32)
            CVA = sd("CVA", [128, 4, SUP], F32)
            CVV = sd("CVV", [128, 2, SUP], F32)
            QT2 = [sd("QT_%d" % i, [128, 2, SUP], BF16) for i in range(2)]
            KT2 = [sd("KT_%d" % i, [128, 2, SUP], BF16) for i in range(2)]
            VCb2 = [sd("VCb_%d" % i, [128, 2, SUP], BF16) for i in range(2)]
            Ktok2 = [sd("Ktok_%d" % i, [128, 2, 4, 128], BF16) for i in range(2)]
            Vtok2 = [sd("Vtok_%d" % i, [128, 2, 4, 128], BF16) for i in range(2)]
            BAt = sd("BAt", [128, 4, 4], F32)
            beta2 = [sd("beta_%d" % i, [128, 4, 2], F32) for i in range(2)]
            gg2 = [sd("gg_%d" % i, [128, 4, 2], F32) for i in range(2)]
            YD = sd("YD", [128, 2, SUP], BF16)
            CH = []
            for ci in range(4):
                nm = lambda n: "%s_%d" % (n, ci)
                Tl = {
                    "banks": (pD[0], pD[1]) if ci % 2 == 0 else (pS[0], pS[1]),
                    "SC": sd(nm("SC"), [128, 6], F32), "bege": sd(nm("bege"), [128, 2], F32),
                    "GTri": sd(nm("GTri"), [128, 2, 128], F32), "DECs": sd(nm("DECs"), [128, 2, 128], F32),
                    "DECi": sd(nm("DECi"), [128, 2, 128], F32), "EGC": sd(nm("EGC"), [128, 2, 128], F32),
                    "Lm": sd(nm("Lm"), [128, 2, 128], BF16),
                    "Pm": [sd(nm("Pm0"), [128, 2, 128], BF16), sd(nm("Pm1"), [128, 2, 128], BF16)],
                    "Qm": [sd(nm("Qm0"), [128, 2, 128], BF16), sd(nm("Qm1"), [128, 2, 128], BF16)],
                    "Xf": sd(nm("Xf"), [128, 2, 128], F32), "Xb": sd(nm("Xb"), [128, 2, 128], BF16),
                    "Yb": sd(nm("Yb"), [128, 2, 128], BF16), "Mm": sd(nm("Mm"), [128, 2, 128], BF16),
                    "Tb": sd(nm("Tb"), [128, 2, 128], BF16), "AU": sd(nm("AU"), [128, 2, 128], BF16),
                    "AW": sd(nm("AW"), [128, 2, 128], BF16), "AT": sd(nm("AT"), [128, 2, 128], BF16),
                    "QDT": sd(nm("QDT"), [128, 2, 128], BF16), "U": sd(nm("U"), [128, 2, 128], F32),
                    "WT": sd(nm("WT"), [128, 2, 128], BF16),
                }
                CH.append(Tl)
            VNf = sd("VNf", [128, 2, 128], F32)
            VNb = sd("VNb", [128, 2, 128], BF16)
            VNSb = sd("VNSb", [128, 2, 128], BF16)
            Sf = sd("Sf", [128, 4, 128], F32)
            Sb_ = sd("Sb", [128, 4, 128], BF16)
            OT = sd("OT", [128, 2, SUP], F32)

            def dn_pass():
                use_cache = hcache["have"]
                hcache["have"] = True
                load_weights(2, NCOL)
                S.op("act", lambda e: e.activation(out=negA.t[:], in_=vecs.t[:, 56:60], func=AF.Exp),
                     reads=[vecs.buf], writes=[negA.buf])
                S.op("dve", lambda e: e.tensor_scalar(out=negA.t[:], in0=negA.t[:], scalar1=-1.0, scalar2=None, op0=ALU.mult),
                     reads=[negA.buf], writes=[negA.buf])
                S.op("dve", lambda e: e.memset(CARRY.t[:], 0.0), writes=[CARRY.buf])
                S.op("dve", lambda e: e.memset(Sf.t[:], 0.0), writes=[Sf.buf])
                S.op("dve", lambda e: e.memset(Sb_.t[:], 0.0), writes=[Sb_.buf])
                def rr_gen(gens):
                    active = list(gens)
                    while active:
                        for gen in list(active):
                            try:
                                next(gen)
                                yield
                            except StopIteration:
                                active.remove(gen)

                def dn_prep(s, g):
                    t0 = s * SUP
                    par = (2 * s + g) % 2
                    QT, KT, VCb, Ktok, Vtok, SZD, beta, gg = QT2[par], KT2[par], VCb2[par], Ktok2[par], Vtok2[par], SZD2[par], beta2[par], gg2[par]
                    if g == 0:
                        if use_cache:
                            hT_load(s, hT, 0)
                            yield
                        else:
                            yield from x_to_hT_gen(s)
                            hT_store(s, hT, 0)
                    gc0 = GCOL * g
                    P = next_pA()
                    for tt in range(4):
                        for c in range(8):
                            S.op("pe", lambda e: e.matmul(P.t[:, tt * 4:tt * 4 + 4], lhsT=hT.t[:, c, tt * 128:(tt + 1) * 128],
                                                          rhs=Wb.t[:, c, gc0 + BA0:gc0 + BA0 + 4], start=(c == 0 and tt == 0), stop=(c == 7)),
                                 reads=[Wb.buf, hT.buf], writes=[P.buf])
                    S.op("dve", lambda e: e.tensor_copy(out=BAt.t[:, :, :], in_=P.t[:, 0:16].rearrange("p (t f) -> p t f", t=4)),
                         reads=[P.buf], writes=[BAt.buf])
                    S.op("act", lambda e: e.activation(out=beta.t[:], in_=BAt.t[:, :, 0:2], func=AF.Exp, scale=-1.0),
                         reads=[BAt.buf], writes=[beta.buf])
                    S.op("dve", lambda e: e.tensor_scalar(out=beta.t[:], in0=beta.t[:], scalar1=1.0, scalar2=None, op0=ALU.add),
                         reads=[beta.buf], writes=[beta.buf])
                    S.op("dve", lambda e: e.reciprocal(out=beta.t[:], in_=beta.t[:]), reads=[beta.buf], writes=[beta.buf])
                    for tt in range(4):
                        S.op("dve", lambda e: e.tensor_tensor(out=gg.t[:, tt, :], in0=BAt.t[:, tt, 2:4], in1=vecs.t[:, 60 + 2 * g:62 + 2 * g], op=ALU.add),
                             reads=[BAt.buf, vecs.buf], writes=[gg.buf])
                    S.op("act", lambda e: e.activation(out=gg.t[:], in_=gg.t[:], func=AF.Exp), reads=[gg.buf], writes=[gg.buf])
                    S.op("act", lambda e: e.activation(out=gg.t[:], in_=gg.t[:], func=AF.Ln, bias=1.0), reads=[gg.buf], writes=[gg.buf])
                    for tt in range(4):
                        S.op("dve", lambda e: e.tensor_tensor(out=gg.t[:, tt, :], in0=gg.t[:, tt, :], in1=negA.t[:, 2 * g:2 * g + 2], op=ALU.mult),
                             reads=[gg.buf, negA.buf], writes=[gg.buf])
                    if s == 0 and g == 0:
                        dump("beta", beta.t[:], [beta.buf])
                        dump("gg", gg.t[:], [gg.buf])
                    yield
                    for j in range(2):
                        P = next_pA()
                        proj_fm(gc0 + ZD0 + j * 128, 128, P)
                        S.op("act", lambda e: e.copy(out=SZD.t[:, j, :], in_=P.t[:, :]), reads=[P.buf], writes=[SZD.buf])
                        yield
                    for cc in range(6):
                        P = next_pA()
                        proj_fm(gc0 + QD0 + cc * 128, 128, P)
                        ci = CIN[cc % 2]
                        vc = 8 + 24 * g + cc * 4
                        S.op("act", lambda e: e.copy(out=ci.t[:, 3:SUP + 3], in_=P.t[:, :]), reads=[P.buf], writes=[ci.buf])
                        S.op("act", lambda e: e.copy(out=ci.t[:, 0:3], in_=CARRY.t[:, 6 * g + cc, :]), reads=[CARRY.buf], writes=[ci.buf])
                        S.op("act", lambda e: e.copy(out=CARRY.t[:, 6 * g + cc, :], in_=ci.t[:, SUP:SUP + 3]), reads=[ci.buf], writes=[CARRY.buf])
                        cvo = CVA[:, cc, :] if cc < 4 else CVV[:, cc - 4, :]
                        S.op("dve", lambda e: e.tensor_scalar(out=cvo.ap, in0=ci.t[:, 0:SUP], scalar1=vecs.t[:, vc:vc + 1],
                                                              scalar2=None, op0=ALU.mult), reads=[ci.buf, vecs.buf], writes=[cvo.buf])
                        for kk in range(1, 4):
                            S.op("dve", lambda e: e.scalar_tensor_tensor(out=cvo.ap, in0=ci.t[:, kk:kk + SUP],
                                                                         scalar=vecs.t[:, vc + kk:vc + kk + 1],
                                                                         in1=cvo.ap, op0=ALU.mult, op1=ALU.add),
                                 reads=[ci.buf, vecs.buf, cvo.buf], writes=[cvo.buf])
                        yield
                    for j in range(2):
                        S.op("act", lambda e: e.activation(out=SZD.t[:, j, :], in_=SZD.t[:, j, :], func=AF.Silu), reads=[SZD.buf], writes=[SZD.buf])
                    for cc in range(4):
                        S.op("act", lambda e: e.activation(out=CVA.t[:, cc, :], in_=CVA.t[:, cc, :], func=AF.Silu), reads=[CVA.buf], writes=[CVA.buf])
                    for j in range(2):
                        S.op("act", lambda e: e.activation(out=VCb.t[:, j, :], in_=CVV.t[:, j, :], func=AF.Silu), reads=[CVV.buf], writes=[VCb.buf])
                    yield
                    if s == 0 and g == 0:
                        for cc in range(4):
                            dump("cv", CVA.t[:, cc, :], [CVA.buf], (cc,))
                    for cc in range(4):
                        j = cc % 2
                        sq = sqt[cc % 2]
                        S.op("act", lambda e: e.activation(out=sq.t[:], in_=CVA.t[:, cc, :], func=AF.Square), reads=[CVA.buf], writes=[sq.buf])
                        P2 = next_pA()
                        S.op("pe", lambda e: e.matmul(P2.t[:, :], lhsT=onesb.t[:, :], rhs=sq.t[:, :], start=True, stop=True),
                             reads=[onesb.buf, sq.buf], writes=[P2.buf])
                        S.op("act", lambda e: e.activation(out=rinv.t[:], in_=P2.t[:, :], func=AF.Ln, bias=EPS), reads=[P2.buf], writes=[rinv.buf])
                        S.op("act", lambda e: e.activation(out=rinv.t[:], in_=rinv.t[:], func=AF.Exp, scale=-0.5), reads=[rinv.buf], writes=[rinv.buf])
                        dstT = QT if cc < 2 else KT
                        scl = 128.0 ** -0.5 if cc < 2 else 1.0
                        S.op("dve", lambda e: e.scalar_tensor_tensor(out=dstT.t[:, j, :], in0=CVA.t[:, cc, :], scalar=scl, in1=rinv.t[:],
                                                                     op0=ALU.mult, op1=ALU.mult),
                             reads=[CVA.buf, rinv.buf], writes=[dstT.buf])
                        yield
                    if s == 0 and g == 0:
                        dump("qt", QT.t[:], [QT.buf])
                        dump("kt", KT.t[:], [KT.buf])
                    for j in range(2):
                        for c in range(4):
                            S.op("pe", lambda e: e.transpose(out=pT.t[:, c * 128:(c + 1) * 128], in_=KT.t[:, j, c * 128:(c + 1) * 128], identity=IDB.ap),
                                 reads=[KT.buf, cbf.buf], writes=[pT.buf])
                            S.op("pe", lambda e: e.transpose(out=pT.t[:, 512 + c * 128:512 + (c + 1) * 128], in_=VCb.t[:, j, c * 128:(c + 1) * 128],
                                                             identity=IDB.ap), reads=[VCb.buf, cbf.buf], writes=[pT.buf])
                        S.op("act", lambda e: e.copy(out=Ktok.t[:, j, :, :], in_=pT.t[:, 0:512].rearrange("p (c t) -> p c t", c=4)),
                             reads=[pT.buf], writes=[Ktok.buf])
                        S.op("dve", lambda e: e.tensor_copy(out=Vtok.t[:, j, :, :], in_=pT.t[:, 512:1024].rearrange("p (c t) -> p c t", c=4)),
                             reads=[pT.buf], writes=[Vtok.buf])
                        yield


                def dn_group(s, g):
                    t0 = s * SUP
                    par = (2 * s + g) % 2
                    QT, KT, VCb, Ktok, Vtok, SZD, beta, gg = QT2[par], KT2[par], VCb2[par], Ktok2[par], Vtok2[par], SZD2[par], beta2[par], gg2[par]
                    def v2(T, a):
                        return T.t[:, a:a + 256].rearrange("p (j i) -> p j i", j=2)

                    def dn_phase1(c, Tl):
                        cs = slice(c * 128, (c + 1) * 128)
                        D0, D1 = Tl["banks"]
                        SC, bege, GTri, DECs, DECi, EGC, Lm, Pm, Qm, Xf, Xb, Yb, Mm, Tb, AU, AW, AT, QDT, U, WT = [Tl[k] for k in (
                            "SC", "bege", "GTri", "DECs", "DECi", "EGC", "Lm", "Pm", "Qm", "Xf", "Xb", "Yb", "Mm", "Tb", "AU", "AW", "AT", "QDT", "U", "WT")]
                        for i3, lhs in enumerate((TRIU, OMT, onesf[:, :])):
                            S.op("pe", lambda e: e.matmul(D0.t[:, 384 + 2 * i3:386 + 2 * i3], lhsT=lhs.ap, rhs=gg.t[:, c, :], start=(i3 == 0), stop=(i3 == 2)),
                                 reads=[lhs.buf, gg.buf], writes=[D0.buf])
                        S.op("act", lambda e: e.activation(out=SC.t[:], in_=D0.t[:, 384:390], func=AF.Exp), reads=[D0.buf], writes=[SC.buf])
                        S.op("dve", lambda e: e.tensor_tensor(out=bege.t[:], in0=beta.t[:, c, :], in1=SC.t[:, 0:2], op=ALU.mult),
                             reads=[beta.buf, SC.buf], writes=[bege.buf])
                        for j in range(2):
                            S.op("dve", lambda e: e.tensor_scalar(out=GTri.t[:, j, :], in0=TRIU.ap, scalar1=gg.t[:, c, j:j + 1], scalar2=None, op0=ALU.mult),
                                 reads=[cf32.buf, gg.buf], writes=[GTri.buf])
                        yield
                        for j in range(2):
                            o = j * 128
                            S.op("pe", lambda e: e.matmul(D0.t[:, o:o + 128], lhsT=GTri.t[:, j, :], rhs=onesf.t[:, :], start=(j == 0), stop=False),
                                 reads=[GTri.buf, onesf.buf], writes=[D0.buf])
                            S.op("pe", lambda e: e.matmul(D0.t[:, o:o + 128], lhsT=negonesf.t[:, :], rhs=GTri.t[:, j, :], start=False, stop=False),
                                 reads=[GTri.buf, negonesf.buf], writes=[D0.buf])
                            S.op("pe", lambda e: e.matmul(D0.t[:, o:o + 128], lhsT=IDB.ap, rhs=MBS.ap, start=False, stop=(j == 1)),
                                 reads=[cbf.buf], writes=[D0.buf])
                        for j in range(2):
                            o = j * 128
                            S.op("pe", lambda e: e.matmul(D1.t[:, o:o + 128], lhsT=onesf.t[:, :], rhs=GTri.t[:, j, :], start=(j == 0), stop=False),
                                 reads=[GTri.buf, onesf.buf], writes=[D1.buf])
                            S.op("pe", lambda e: e.matmul(D1.t[:, o:o + 128], lhsT=GTri.t[:, j, :], rhs=negonesf.t[:, :], start=False, stop=False),
                                 reads=[GTri.buf, negonesf.buf], writes=[D1.buf])
                            S.op("pe", lambda e: e.matmul(D1.t[:, o:o + 128], lhsT=IDB.ap, rhs=MBI.ap, start=False, stop=False),
                                 reads=[cbf.buf], writes=[D1.buf])
                            S.op("pe", lambda e: e.matmul(D1.t[:, 256 + o:256 + o + 128], lhsT=onesf.t[:, :], rhs=GTri.t[:, j, :], start=False, stop=(j == 1)),
                                 reads=[GTri.buf, onesf.buf], writes=[D1.buf])
                        S.op("act", lambda e: e.activation(out=DECs.t[:, :, :], in_=v2(D0, 0), func=AF.Exp), reads=[D0.buf], writes=[DECs.buf])
                        S.op("act", lambda e: e.activation(out=DECi.t[:, :, :], in_=v2(D1, 0), func=AF.Exp), reads=[D1.buf], writes=[DECi.buf])
                        S.op("act", lambda e: e.activation(out=EGC.t[:, :, :], in_=v2(D1, 256), func=AF.Exp), reads=[D1.buf], writes=[EGC.buf])
                        yield
                        for j in range(2):
                            o = j * 128
                            S.op("pe", lambda e: e.matmul(D0.t[:, o:o + 128], lhsT=KT.t[:, j, cs], rhs=KT.t[:, j, cs], start=(j == 0), stop=(j == 1)),
                                 reads=[KT.buf], writes=[D0.buf])
                        for j in range(2):
                            o = j * 128
                            S.op("pe", lambda e: e.matmul(D1.t[:, o:o + 128], lhsT=KT.t[:, j, cs], rhs=QT.t[:, j, cs], start=(j == 0), stop=(j == 1)),
                                 reads=[KT.buf, QT.buf], writes=[D1.buf])
                        for j in range(2):
                            o = j * 128
                            S.op("dve", lambda e: e.scalar_tensor_tensor(out=Lm.t[:, j, :], in0=D0.t[:, o:o + 128], scalar=beta.t[:, c, j:j + 1],
                                                                         in1=DECs.t[:, j, :], op0=ALU.mult, op1=ALU.mult),
                                 reads=[D0.buf, beta.buf, DECs.buf], writes=[Lm.buf])
                        S.op("dve", lambda e: e.tensor_tensor(out=AT.t[:, :, :], in0=v2(D1, 0), in1=DECi.t[:, :, :], op=ALU.mult),
                             reads=[D1.buf, DECi.buf], writes=[AT.buf])
                        S.op("dve", lambda e: e.tensor_tensor(out=QDT.t[:, :, :], in0=QT.t[:, :, cs], in1=EGC.t[:, :, :], op=ALU.mult),
                             reads=[QT.buf, EGC.buf], writes=[QDT.buf])
                        for j in range(2):
                            S.op("dve", lambda e: e.tensor_tensor(out=Pm[0].t[:, j, :], in0=Lm.t[:, j, :], in1=M32.ap, op=ALU.mult),
                                 reads=[Lm.buf, cbf.buf], writes=[Pm[0].buf])
                        yield
                        for j in range(2):
                            S.op("pe", lambda e: e.transpose(out=pT.t[:, j * 128:(j + 1) * 128], in_=Lm.t[:, j, :], identity=IDB.ap),
                                 reads=[Lm.buf, cbf.buf], writes=[pT.buf])
                        for j in range(2):
                            S.op("pe", lambda e: e.transpose(out=pT.t[:, 256 + j * 128:256 + (j + 1) * 128], in_=Pm[0].t[:, j, :], identity=IDB.ap),
                                 reads=[Pm[0].buf, cbf.buf], writes=[pT.buf])
                        S.op("act", lambda e: e.copy(out=Mm.t[:, :, :], in_=pT.t[:, 0:256].rearrange("p (j i) -> p j i", j=2)), reads=[pT.buf], writes=[Mm.buf])
                        S.op("act", lambda e: e.copy(out=Qm[0].t[:, :, :], in_=pT.t[:, 256:512].rearrange("p (j i) -> p j i", j=2)), reads=[pT.buf], writes=[Qm[0].buf])
                        for j in range(2):
                            S.op("dve", lambda e: e.tensor_tensor(out=Xb.t[:, j, :], in0=IDB.ap, in1=pT.t[:, 256 + j * 128:256 + (j + 1) * 128], op=ALU.subtract),
                                 reads=[cbf.buf, pT.buf], writes=[Xb.buf])
                            S.op("dve", lambda e: e.tensor_tensor(out=Yb.t[:, j, :], in0=IDB.ap, in1=Pm[0].t[:, j, :], op=ALU.subtract),
                                 reads=[cbf.buf, Pm[0].buf], writes=[Yb.buf])
                        yield
                        Pc, Qc = Pm[0], Qm[0]
                        for kstep in range(1, 6):
                            Pn, Qn = Pm[kstep % 2], Qm[kstep % 2]
                            if kstep >= 2:
                                for j in range(2):
                                    o = j * 128
                                    S.op("pe", lambda e: e.matmul(D1.t[:, o:o + 128], lhsT=Pc.t[:, j, :], rhs=Xb.t[:, j, :], start=(j == 0), stop=False),
                                         reads=[Pc.buf, Xb.buf], writes=[D1.buf])
                                    S.op("pe", lambda e: e.matmul(D1.t[:, 256 + o:256 + o + 128], lhsT=Qc.t[:, j, :], rhs=Yb.t[:, j, :], start=False, stop=(j == 1)),
                                         reads=[Qc.buf, Yb.buf], writes=[D1.buf])
                            if kstep <= 4:
                                for j in range(2):
                                    o = j * 128
                                    S.op("pe", lambda e: e.matmul(D0.t[:, o:o + 128], lhsT=Qc.t[:, j, :], rhs=Pc.t[:, j, :], start=(j == 0), stop=False),
                                         reads=[Qc.buf, Pc.buf], writes=[D0.buf])
                                    S.op("pe", lambda e: e.matmul(D0.t[:, 256 + o:256 + o + 128], lhsT=Pc.t[:, j, :], rhs=Qc.t[:, j, :], start=False, stop=(j == 1)),
                                         reads=[Qc.buf, Pc.buf], writes=[D0.buf])
                            if kstep >= 2:
                                S.op("dve", lambda e: e.tensor_tensor(out=Xb.t[:, :, :], in0=v2(D1, 0), in1=Xb.t[:, :, :], op=ALU.add),
                                     reads=[D1.buf, Xb.buf], writes=[Xb.buf])
                                S.op("dve", lambda e: e.tensor_tensor(out=Yb.t[:, :, :], in0=v2(D1, 256), in1=Yb.t[:, :, :], op=ALU.add),
                                     reads=[D1.buf, Yb.buf], writes=[Yb.buf])
                            if kstep <= 4:
                                S.op("act", lambda e: e.copy(out=Pn.t[:, :, :], in_=v2(D0, 0)), reads=[D0.buf], writes=[Pn.buf])
                                S.op("act", lambda e: e.copy(out=Qn.t[:, :, :], in_=v2(D0, 256)), reads=[D0.buf], writes=[Qn.buf])
                                Pc, Qc = Pn, Qn
                            yield
                        for lvl, MB in ((1, MB1), (2, MB2)):
                            for j in range(2):
                                o = j * 128
                                S.op("pe", lambda e: e.matmul(D0.t[:, o:o + 128], lhsT=Mm.t[:, j, :], rhs=Yb.t[:, j, :], start=(j == 0), stop=(j == 1)),
                                     reads=[Mm.buf, Yb.buf], writes=[D0.buf])
                            for j in range(2):
                                o = j * 128
                                S.op("dve", lambda e: e.tensor_tensor(out=Tb.t[:, j, :], in0=D0.t[:, o:o + 128], in1=MB.ap, op=ALU.mult),
                                     reads=[D0.buf, cbf.buf], writes=[Tb.buf])
                            yield
                            for j in range(2):
                                o = j * 128
                                S.op("pe", lambda e: e.matmul(D1.t[:, o:o + 128], lhsT=Tb.t[:, j, :], rhs=Xb.t[:, j, :], start=(j == 0), stop=(lvl == 2 and j == 1)),
                                     reads=[Tb.buf, Xb.buf], writes=[D1.buf])
                                if lvl == 1:
                                    S.op("pe", lambda e: e.matmul(D1.t[:, 256 + o:256 + o + 128], lhsT=Xb.t[:, j, :], rhs=Tb.t[:, j, :], start=False, stop=(j == 1)),
                                         reads=[Tb.buf, Xb.buf], writes=[D1.buf])
                            if lvl == 1:
                                S.op("dve", lambda e: e.tensor_tensor(out=Xb.t[:, :, :], in0=Xb.t[:, :, :], in1=v2(D1, 0), op=ALU.subtract),
                                     reads=[D1.buf, Xb.buf], writes=[Xb.buf])
                                S.op("dve", lambda e: e.tensor_tensor(out=Yb.t[:, :, :], in0=Yb.t[:, :, :], in1=v2(D1, 256), op=ALU.subtract),
                                     reads=[D1.buf, Yb.buf], writes=[Yb.buf])
                            else:
                                S.op("dve", lambda e: e.tensor_tensor(out=Xf.t[:, :, :], in0=Xb.t[:, :, :], in1=v2(D1, 0), op=ALU.subtract),
                                     reads=[D1.buf, Xb.buf], writes=[Xf.buf])
                                for j in range(2):
                                    S.op("dve", lambda e: e.tensor_scalar(out=AU.t[:, j, :], in0=Xf.t[:, j, :], scalar1=beta.t[:, c, j:j + 1], scalar2=None, op0=ALU.mult),
                                         reads=[Xf.buf, beta.buf], writes=[AU.buf])
                                    S.op("dve", lambda e: e.tensor_scalar(out=AW.t[:, j, :], in0=Xf.t[:, j, :], scalar1=bege.t[:, j:j + 1], scalar2=None, op0=ALU.mult),
                                         reads=[Xf.buf, bege.buf], writes=[AW.buf])
                            yield
                        for j in range(2):
                            o = j * 128
                            S.op("pe", lambda e: e.matmul(D0.t[:, o:o + 128], lhsT=AU.t[:, j, :], rhs=Vtok.t[:, j, c, :], start=(j == 0), stop=False),
                                 reads=[AU.buf, Vtok.buf], writes=[D0.buf])
                            S.op("pe", lambda e: e.matmul(D0.t[:, 256 + o:256 + o + 128], lhsT=Ktok.t[:, j, c, :], rhs=AW.t[:, j, :], start=False, stop=(j == 1)),
                                 reads=[AW.buf, Ktok.buf], writes=[D0.buf])
                        S.op("act", lambda e: e.copy(out=U.t[:, :, :], in_=v2(D0, 0)), reads=[D0.buf], writes=[U.buf])
                        S.op("dve", lambda e: e.tensor_copy(out=WT.t[:, :, :], in_=v2(D0, 256)), reads=[D0.buf], writes=[WT.buf])
                        yield

                    def dn_phase2(c, Tl):
                        cs = slice(c * 128, (c + 1) * 128)
                        SC, AT, QDT, U, WT = Tl["SC"], Tl["AT"], Tl["QDT"], Tl["U"], Tl["WT"]
                        R = pO
                        for j in range(2):
                            o = j * 128
                            S.op("pe", lambda e: e.matmul(R.t[:, o:o + 128], lhsT=WT.t[:, j, :], rhs=Sb_.t[:, 2 * g + j, :], start=(j == 0), stop=(j == 1)),
                                 reads=[WT.buf, Sb_.buf], writes=[R.buf])
                        S.op("dve", lambda e: e.tensor_tensor(out=VNf.t[:, :, :], in0=U.t[:, :, :], in1=v2(R, 0), op=ALU.subtract),
                             reads=[U.buf, R.buf], writes=[VNf.buf])
                        S.op("act", lambda e: e.copy(out=VNb.t[:, :, :], in_=VNf.t[:, :, :]), reads=[VNf.buf], writes=[VNb.buf])
                        for j in range(2):
                            S.op("dve", lambda e: e.tensor_scalar(out=VNSb.t[:, j, :], in0=VNf.t[:, j, :], scalar1=SC.t[:, 2 + j:3 + j], scalar2=None, op0=ALU.mult),
                                 reads=[VNf.buf, SC.buf], writes=[VNSb.buf])
                        yield
                        for j in range(2):
                            o = j * 128
                            S.op("pe", lambda e: e.matmul(R.t[:, 256 + o:256 + o + 128], lhsT=Sb_.t[:, 2 * g + j, :], rhs=QDT.t[:, j, :], start=(j == 0), stop=False),
                                 reads=[Sb_.buf, QDT.buf], writes=[R.buf])
                            S.op("pe", lambda e: e.matmul(R.t[:, 256 + o:256 + o + 128], lhsT=VNb.t[:, j, :], rhs=AT.t[:, j, :], start=False, stop=False),
                                 reads=[VNb.buf, AT.buf], writes=[R.buf])
                        for j in range(2):
                            o = j * 128
                            S.op("pe", lambda e: e.matmul(R.t[:, o:o + 128], lhsT=Ktok.t[:, j, c, :], rhs=VNSb.t[:, j, :], start=False, stop=(j == 1)),
                                 reads=[Ktok.buf, VNSb.buf], writes=[R.buf])
                        S.op("act", lambda e: e.copy(out=OT.t[:, :, cs], in_=v2(R, 256)), reads=[R.buf], writes=[OT.buf])
                        for j in range(2):
                            o = j * 128
                            S.op("dve", lambda e: e.scalar_tensor_tensor(out=Sf.t[:, 2 * g + j, :], in0=Sf.t[:, 2 * g + j, :], scalar=SC.t[:, 4 + j:5 + j], in1=R.t[:, o:o + 128],
                                                                         op0=ALU.mult, op1=ALU.add), reads=[Sf.buf, SC.buf, R.buf], writes=[Sf.buf])
                        S.op("act", lambda e: e.copy(out=Sb_.t[:, 2 * g:2 * g + 2, :], in_=Sf.t[:, 2 * g:2 * g + 2, :]), reads=[Sf.buf], writes=[Sb_.buf])
                        yield

                    def seq(*gens):
                        for gen in gens:
                            yield from gen

                    def run_rr(gens):
                        active = list(gens)
                        while active:
                            for gen in list(active):
                                try:
                                    next(gen)
                                except StopIteration:
                                    active.remove(gen)

                    yield from rr_gen([dn_phase1(cc_, CH[cc_]) for cc_ in range(4)])
                    for cc_ in range(4):
                        yield from dn_phase2(cc_, CH[cc_])
                    if "od" in dbg_d:
                        for j in range(2):
                            dump("od", OT.t[:, j, :], [OT.buf], (2 * g + j, slice(None), slice(t0, t0 + SUP)))
                    for j in range(2):
                        sq = sqt[j]
                        S.op("act", lambda e: e.activation(out=sq.t[:], in_=OT.t[:, j, :], func=AF.Square), reads=[OT.buf], writes=[sq.buf])
                        P2 = next_pA()
                        S.op("pe", lambda e: e.matmul(P2.t[:, :], lhsT=onesb.t[:, :], rhs=sq.t[:, :], start=True, stop=True),
                             reads=[onesb.buf, sq.buf], writes=[P2.buf])
                        S.op("act", lambda e: e.activation(out=rinv.t[:], in_=P2.t[:, :], func=AF.Ln, scale=1.0 / 128.0, bias=EPS),
                             reads=[P2.buf], writes=[rinv.buf])
                        S.op("act", lambda e: e.activation(out=rinv.t[:], in_=rinv.t[:], func=AF.Exp, scale=-0.5), reads=[rinv.buf], writes=[rinv.buf])
                        S.op("dve", lambda e: e.scalar_tensor_tensor(out=qraw.t[:], in0=OT.t[:, j, :], scalar=vecs.t[:, 64:65], in1=rinv.t[:],
                                                                     op0=ALU.mult, op1=ALU.mult), reads=[OT.buf, vecs.buf, rinv.buf], writes=[qraw.buf])
                        S.op("dve", lambda e: e.tensor_tensor(out=YD.t[:, j, :], in0=qraw.t[:], in1=SZD.t[:, j, :], op=ALU.mult),
                             reads=[qraw.buf, SZD.buf], writes=[YD.buf])
                        yield
                    dma(ys_d[512 + 256 * g:512 + 256 * g + 256, t0:t0 + SUP].rearrange("(h p) t -> p h t", h=2), YD.t[:, :, :], d_ys, reads=[YD.buf], writes=[ysb], eng="sp")


                groups = [(s_, g_) for s_ in range(n_super) for g_ in range(2)]
                run_rr([dn_prep(*groups[0])])
                for k_, (s_, g_) in enumerate(groups):
                    run_rr([dn_group(s_, g_)] + ([dn_prep(*groups[k_ + 1])] if k_ + 1 < len(groups) else []))

            if "D" in passes:
                dn_pass()
            S.barrier()

        if "ys" in dbg_d:
            with ExitStack() as stY:
                d_y2 = S.dma_sem("y2")
                ytmp = sb("ytmp", [128, 8, S_LOC], BF16, stack=stY)
                dma(ytmp.t[:, :, :], ys_d[:, :].rearrange("(c p) t -> p c t", p=128), d_y2, reads=[ysb], writes=[ytmp.buf])
                dump("ys", ytmp.t[:, :, :], [ytmp.buf])
                S.barrier()

        if do_out:
            with ExitStack() as stO:
                so_ = lambda name, shape, dt: sb(name, shape, dt, stack=stO)
                WOb = so_("WOb", [128, 8, D], BF16)
                k = 0
                for c in range(8):
                    t = xt[k % 2]
                    dma(t.t[:, :], wout_d[c * 128:(c + 1) * 128, :], d_xt[k % 2], writes=[t.buf])
                    S.op("dve", lambda e: e.tensor_copy(out=WOb.t[:, c, :], in_=t.t[:, :]), reads=[t.buf], writes=[WOb.buf])
                    k += 1
                YG = [so_("YG0", [128, 8, SUP], BF16), so_("YG1", [128, 8, SUP], BF16)]
                d_yg = [S.dma_sem("yg0"), S.dma_sem("yg1")]
                OB = [so_("OB0", [128, D], F32), so_("OB1", [128, D], F32)]
                d_ob = [S.dma_sem("ob0"), S.dma_sem("ob1")]
                outb = Buf("out")
                for s in range(n_super):
                    t0 = s * SUP
                    yg = YG[s % 2]
                    dma(yg.t[:, :, :], ys_d[:, t0:t0 + SUP].rearrange("(c p) t -> p c t", p=128), d_yg[s % 2], reads=[ysb], writes=[yg.buf])
                    for tt in range(4):
                        r0 = t0 + tt * 128
                        i2 = (4 * s + tt) % 2
                        X = xt[i2]
                        dma(X.t[:, :], x_d[r0:r0 + 128, :], d_xt[i2], writes=[X.buf])
                        for half in range(2):
                            P = next_pA()
                            for c in range(8):
                                S.op("pe", lambda e: e.matmul(P.t[:, :], lhsT=yg.t[:, c, tt * 128:(tt + 1) * 128], rhs=WOb.t[:, c, half * 512:(half + 1) * 512],
                                                              start=(c == 0), stop=(c == 7)), reads=[yg.buf, WOb.buf], writes=[P.buf])
                            S.op("dve", lambda e: e.tensor_tensor(out=OB[i2].t[:, half * 512:(half + 1) * 512], in0=P.t[:, :],
                                                                  in1=X.t[:, half * 512:(half + 1) * 512], op=ALU.add),
                                 reads=[P.buf, X.buf], writes=[OB[i2].buf])
                        dma(out_d[r0:r0 + 128, :], OB[i2].t[:, :], d_ob[i2], reads=[OB[i2].buf], writes=[outb], eng="act")
                for dsm in d_ob:
                    if dsm.count:
                        nc.sync.wait_ge(dsm.h, dsm.count * 16)
                S.barrier()
        if d_dbg.count:
            nc.sync.wait_ge(d_dbg.h, d_dbg.count * 16)
        if d_ys.count:
            nc.sync.wait_ge(d_ys.h, d_ys.count * 16)
    return nc


def make_in_maps(x, rel_bias, norm_w, w_in, q_norm_w, k_norm_w, conv_w, a_log, dt_bias, dn_norm_w, w_out, cores=range(4)):
    cf32, cbf, onehot, sel = host_constants()
    f32 = np.float32
    x = np.asarray(x, dtype=f32)
    w_in0 = np.asarray(w_in, dtype=f32)[0]
    w_out0 = np.ascontiguousarray(np.asarray(w_out, dtype=f32)[0])
    conv0 = np.asarray(conv_w, dtype=f32)[0]
    w_in_c = np.zeros((3, D, NCOL), dtype=f32)
    vecs = np.zeros((3, 128, 72), dtype=f32)
    relb = np.empty((2, 32, 4), dtype=f32)
    nw = np.asarray(norm_w, dtype=f32)[0].reshape(8, 128).T
    for hh in range(2):
        cols = np.concatenate([np.arange(256 * hh, 256 * hh + 256) + base for base in (0, 512, 1024, 1536)])
        w_in_c[hh, :, 0:1024] = w_in0[:, cols]
        vecs[hh, :, 0:8] = nw
        vecs[hh, 0:64, 8] = np.asarray(q_norm_w, dtype=f32)[0]
        vecs[hh, 0:64, 9] = np.asarray(k_norm_w, dtype=f32)[0]
        relb[hh] = np.asarray(rel_bias, dtype=f32)[:, 4 * hh:4 * hh + 4]
    vecs[2, :, 0:8] = nw
    for g in range(2):
        cols = np.concatenate([np.arange(256 * g, 256 * g + 256) + base for base in (2048, 2560, 3072, 3584)]
                              + [np.array([4096 + 2 * g, 4097 + 2 * g, 4100 + 2 * g, 4101 + 2 * g])])
        w_in_c[2, :, GCOL * g:GCOL * g + GCOL] = w_in0[:, cols]
        for cc in range(6):
            ch0 = (cc // 2) * 512 + 256 * g + (cc % 2) * 128
            vecs[2, :, 8 + 24 * g + cc * 4:8 + 24 * g + cc * 4 + 4] = conv0[:, ch0:ch0 + 128].T
    vecs[2, :, 56:60] = np.asarray(a_log, dtype=f32)[0][None, :]
    vecs[2, :, 60:64] = np.asarray(dt_bias, dtype=f32)[0][None, :]
    vecs[2, :, 64] = np.asarray(dn_norm_w, dtype=f32)[0]
    in_maps = []
    for b in cores:
        in_maps.append({
            "x": np.ascontiguousarray(x[b]),
            "w_in": w_in_c, "w_out": w_out0, "vecs": vecs, "relb": relb,
            "cf32": cf32, "cbf": cbf, "onehot": onehot, "sel": sel,
        })
    return in_maps


def kernel(x, rel_bias, norm_w, w_in, q_norm_w, k_norm_w, conv_w, a_log, dt_bias, dn_norm_w, w_out):
    in_maps = make_in_maps(x, rel_bias, norm_w, w_in, q_norm_w, k_norm_w, conv_w, a_log, dt_bias, dn_norm_w, w_out)
    nc = build_program(n_super=S_TOT // SUP)
    res = run_bass_kernel_spmd(nc, in_maps, core_ids=list(range(4)))
    out = np.empty((4, S_TOT, D), dtype=np.float32)
    for b in range(4):
        out[b] = res.results[b]["out"]
    return out
```

```python
import math
from contextlib import ExitStack

import numpy as np
import ml_dtypes
import concourse.bass as bass
import concourse.mybir as mybir
from concourse.bass_utils import run_bass_kernel_spmd

F32 = mybir.dt.float32
BF16 = mybir.dt.bfloat16
AF = mybir.ActivationFunctionType
ALU = mybir.AluOpType
AX = mybir.AxisListType

NCORES = 8
S_TOT = 8192
D = 1024
SUP = 512
NCOL = 2056
QA0, KA0, VA0, ZA0 = 0, 256, 512, 768
QD0, KD0, VD0, ZD0, BA0, GCOL = 0, 256, 512, 768, 1024, 1028
EPS = 1e-6
DBG_ONE = False
SEQ_DEBUG = False
NEG = -30000.0


class SemObj:
    def __init__(self, h, scale):
        self.h = h
        self.scale = scale
        self.count = 0


class Buf:
    def __init__(self, name):
        self.name = name
        self.w = None
        self.r = {}


class Opnd:
    def __init__(self, ap, buf):
        self.ap = ap
        self.buf = buf


class Tile:
    def __init__(self, t, name, excl=False):
        self.t = t
        self.buf = Buf(name)
        self.excl = excl

    def __getitem__(self, idx):
        return Opnd(self.t[idx], self.buf)

    def v(self, idx, buf):
        return Opnd(self.t[idx], buf)


class PEProxy:
    def __init__(self, eng):
        self._eng = eng

    def matmul(self, *a, **k):
        k.setdefault("skip_group_check", True)
        return self._eng.matmul(*a, **k)

    def __getattr__(self, name):
        return getattr(self._eng, name)


class Sched:
    def __init__(self, nc, stack):
        self.nc = nc
        self.stack = stack
        self.engs = {"pe": PEProxy(nc.tensor), "act": nc.scalar, "dve": nc.vector, "pool": nc.gpsimd, "sp": nc.sync}
        self.sem = {k: SemObj(stack.enter_context(nc.semaphore("s_" + k)), 1) for k in self.engs}
        self.seen = {k: {} for k in self.engs}
        self.psum_bufs = set()
        self.n_ops = 0
        self.all_sems = list(self.sem.values())

    def dma_sem(self, name):
        so = SemObj(self.stack.enter_context(self.nc.semaphore("d_" + name)), 16)
        self.all_sems.append(so)
        return so

    def barrier(self):
        for e, eng in self.engs.items():
            seen = self.seen[e]
            for so in self.all_sems:
                if so.count > seen.get(so, 0) and not (so is self.sem[e]):
                    eng.wait_ge(so.h, so.count * so.scale)
                    seen[so] = so.count

    def op(self, e, fn, reads=(), writes=(), dsem=None):
        deps = {}
        own = self.sem[e]

        def add(so, c):
            if so is own and e == "pe" and dsem is None:
                return
            if deps.get(so, 0) < c:
                deps[so] = c

        rb, wb = [], []
        for o in reads:
            b = o.buf if isinstance(o, Opnd) else o
            (wb if b in self.psum_bufs else rb).append(b)
        for o in writes:
            wb.append(o.buf if isinstance(o, Opnd) else o)
        for b in rb:
            if b.w:
                add(*b.w)
        for b in wb:
            if b.w:
                add(*b.w)
            for so, c in b.r.items():
                add(so, c)
        seen = self.seen[e]
        eng = self.engs[e]
        for so, c in deps.items():
            if seen.get(so, 0) < c:
                eng.wait_ge(so.h, c * so.scale)
                seen[so] = c
        ins = fn(eng)
        so = dsem if dsem is not None else own
        so.count += 1
        ins.then_inc(so.h, so.scale)
        for b in rb:
            if b.r.get(so, 0) < so.count:
                b.r[so] = so.count
        for b in wb:
            b.w = (so, so.count)
            b.r = {}
        self.n_ops += 1
        return ins


def t5_bucket_np(n):
    if n < 16:
        return n
    v = np.float32(np.log(np.float32(n) / np.float32(16.0))) / np.float32(math.log(8.0)) * np.float32(16.0)
    return min(31, 16 + int(v))


def host_constants():
    bf = ml_dtypes.bfloat16
    ident = np.eye(128, dtype=np.float32)
    jflip = ident[::-1].copy()
    m = np.arange(128)[:, None]
    i = np.arange(128)[None, :]
    triu = (m <= i).astype(np.float32)
    omt = 1.0 - triu
    mbs = np.where(i >= m, NEG, 0.0).astype(np.float32)
    mbi = np.where(m > i, NEG, 0.0).astype(np.float32)
    cf32 = np.concatenate([jflip, triu, omt, ident], axis=1).astype(np.float32)
    m32 = (m // 32 == i // 32).astype(np.float32)
    m64 = (m // 64 == i // 64).astype(np.float32)
    cbf = np.concatenate([ident, mbs, mbi, m32, m64 - m32, 1.0 - m64], axis=1).astype(bf)
    onehot = np.zeros((32, S_TOT), dtype=np.float32)
    onehot[np.arange(S_TOT) // 256, np.arange(S_TOT)] = 1.0
    sel = np.zeros((33, 384), dtype=np.float32)
    for idx in range(383):
        dist = idx - 127
        if dist < 0:
            sel[32, idx] = NEG
        else:
            sel[t5_bucket_np(dist), idx] += 1.0
            sel[31, idx] -= 1.0
    return cf32, cbf, onehot.astype(bf), sel


def build_program(n_super=16, dbg=None, do_out=True, passes=("A0", "A1", "D")):
    dbg = dbg or []
    S_LOC = n_super * SUP
    nc = bass.Bass("TRN2", target_bir_lowering=False)
    x_d = nc.dram_tensor("x", [S_TOT, D], F32, kind="ExternalInput").ap()
    win_d = nc.dram_tensor("w_in", [3, D, NCOL], F32, kind="ExternalInput").ap()
    wout_d = nc.dram_tensor("w_out", [D, D], F32, kind="ExternalInput").ap()
    vecs_d = nc.dram_tensor("vecs", [3, 128, 72], F32, kind="ExternalInput").ap()
    relb_d = nc.dram_tensor("relb", [2, 32, 4], F32, kind="ExternalInput").ap()
    cf32_d = nc.dram_tensor("cf32", [128, 512], F32, kind="ExternalInput").ap()
    cbf_d = nc.dram_tensor("cbf", [128, 768], BF16, kind="ExternalInput").ap()
    oneh_d = nc.dram_tensor("onehot", [32, S_TOT], BF16, kind="ExternalInput").ap()
    sel_d = nc.dram_tensor("sel", [33, 384], F32, kind="ExternalInput").ap()
    out_d = nc.dram_tensor("out", [S_TOT, D], F32, kind="ExternalOutput").ap()
    fscr_d = nc.dram_tensor("fscr", [4, 384], F32, kind="Internal").ap()
    ys_d = nc.dram_tensor("ys", [1024, S_LOC], BF16, kind="Internal").ap()
    hts_d = nc.dram_tensor("hts", [1024, S_LOC], BF16, kind="Internal").ap()
    dbg_d = {}
    for name, shape, dt in dbg:
        dbg_d[name] = nc.dram_tensor("dbg_" + name, shape, dt, kind="ExternalOutput").ap()

    with ExitStack() as st:
        S = Sched(nc, st)

        def sb(name, shape, dt, stack=None):
            return Tile((stack or st).enter_context(nc.sbuf_tensor("sb_" + name, shape, dt)), name)

        def ps(name, shape, dt):
            t = Tile(st.enter_context(nc.psum_tensor("ps_" + name, shape, dt)), name, excl=True)
            S.psum_bufs.add(t.buf)
            return t

        pT = ps("pT", [128, 1024], BF16)
        pA = [ps("pA0", [128, 512], F32), ps("pA1", [128, 512], F32)]
        pS = [ps("pS0", [128, 512], F32), ps("pS1", [128, 512], F32)]
        pO = ps("pO", [128, 512], F32)
        pD = [ps("pD0", [128, 512], F32), ps("pD1", [128, 512], F32)]

        vecs = sb("vecs", [128, 72], F32)
        cf32 = sb("cf32", [128, 512], F32)
        cbf = sb("cbf", [128, 768], BF16)
        onesf = sb("onesf", [128, 128], F32)
        negonesf = sb("negonesf", [128, 128], F32)
        onesb = sb("onesb", [128, 128], BF16)
        Wb = sb("Wb", [128, 8, NCOL], BF16)
        xt = [sb("xt0", [128, 1024], F32), sb("xt1", [128, 1024], F32)]
        d_xt = [S.dma_sem("xt0"), S.dma_sem("xt1")]
        xb = sb("xb", [128, 1024], BF16)
        junk = sb("junk", [128, 1024], BF16)
        ssq = sb("ssq", [128, 2], F32)
        hT = sb("hT", [128, 8, SUP], BF16)
        sqt = [sb("sqt0", [128, SUP], BF16), sb("sqt1", [128, SUP], BF16)]
        qraw = sb("qraw", [128, SUP], F32)
        rinv = sb("rinv", [128, SUP], F32)

        JF = cf32[:, 0:128]
        TRIU = cf32[:, 128:256]
        OMT = cf32[:, 256:384]
        IDF = cf32[:, 384:512]
        IDB = cbf[:, 0:128]
        MBS = cbf[:, 128:256]
        MBI = cbf[:, 256:384]
        M32 = cbf[:, 384:512]
        MB1 = cbf[:, 512:640]
        MB2 = cbf[:, 640:768]

        def dma(out, in_, sem, reads=(), writes=(), eng="sp"):
            return S.op(eng, lambda e: e.dma_start(out=out, in_=in_), reads=reads, writes=writes, dsem=sem)

        d_const = S.dma_sem("const")
        d_vecs = S.dma_sem("vecs")
        d_t5 = S.dma_sem("t5")
        d_t5b = S.dma_sem("t5b")
        d_fs = S.dma_sem("fs")
        d_hk = S.dma_sem("hk")
        d_dbg = S.dma_sem("dbg")
        d_ys = S.dma_sem("ys")
        fscr = Buf("fscr")
        ysb = Buf("ys")
        d_hts = [S.dma_sem("hts0"), S.dma_sem("hts1")]
        d_hld = [S.dma_sem("hld0"), S.dma_sem("hld1")]
        htsb = [Buf("hts_%d" % i) for i in range(n_super)]
        hcache = {"have": False}

        def hT_store(s, hTd, slot):
            t0 = s * SUP
            dma(hts_d[:, t0:t0 + SUP].rearrange("(c p) t -> p c t", p=128), hTd.t[:, :, :], d_hts[slot], reads=[hTd.buf], writes=[htsb[s]])

        def hT_load(s, hTd, slot):
            t0 = s * SUP
            dma(hTd.t[:, :, :], hts_d[:, t0:t0 + SUP].rearrange("(c p) t -> p c t", p=128), d_hld[slot], reads=[htsb[s]], writes=[hTd.buf])

        dma(cf32.t[:], cf32_d[:, :], d_const, writes=[cf32.buf])
        dma(cbf.t[:], cbf_d[:, :], S.dma_sem("cbf"), writes=[cbf.buf])
        S.op("dve", lambda e: e.memset(onesf.t[:], 1.0), writes=[onesf.buf])
        S.op("dve", lambda e: e.memset(negonesf.t[:], -1.0), writes=[negonesf.buf])
        S.op("dve", lambda e: e.memset(onesb.t[:], 1.0), writes=[onesb.buf])

        def dump(name, src_ap, bufs, dst_idx=None):
            if name in dbg_d:
                dst = dbg_d[name] if dst_idx is None else dbg_d[name][dst_idx]
                dma(dst, src_ap, d_dbg, reads=bufs, eng="pool")

        pa_i = [0]

        def next_pA():
            pa_i[0] += 1
            return pA[pa_i[0] % 2]

        xk = [0]
        ev_i = [0]

        def evac_eng():
            ev_i[0] += 1
            return "act" if ev_i[0] % 2 == 0 else "dve"

        def load_weights(pi, ncols):
            dma(vecs.t[:], vecs_d[pi, :, :], d_vecs, writes=[vecs.buf])
            k = 0
            for c in range(8):
                for c0 in range(0, ncols, 1024):
                    c1 = min(ncols, c0 + 1024)
                    t = xt[k % 2]
                    dma(t.t[:, 0:c1 - c0], win_d[pi, c * 128:(c + 1) * 128, c0:c1], d_xt[k % 2], writes=[t.buf])
                    S.op("dve",
                         lambda e: e.tensor_scalar(out=Wb.t[:, c, c0:c1], in0=t.t[:, 0:c1 - c0], scalar1=vecs.t[:, c:c + 1],
                                                   scalar2=None, op0=ALU.mult), reads=[t.buf, vecs.buf], writes=[Wb.buf])
                    k += 1

        hTcur = [hT]

        def x_to_hT_gen(s, hTd=None, sq_eng="act"):
            hTd = hTd or hT
            t0 = s * SUP
            for tt in range(4):
                r0 = t0 + tt * 128
                xi = xk[0] % 2
                xk[0] += 1
                X = xt[xi]
                dma(X.t[:], x_d[r0:r0 + 128, :], d_xt[xi], writes=[X.buf])
                if sq_eng == "dve":
                    S.op("dve", lambda e: e.scalar_tensor_tensor(out=junk.t[:], in0=X.t[:], scalar=1.0, in1=X.t[:], op0=ALU.mult, op1=ALU.mult,
                                                                 accum_out=ssq.t[:, 0:1]),
                         reads=[X.buf], writes=[junk.buf, ssq.buf])
                else:
                    S.op("act", lambda e: e.activation(out=junk.t[:], in_=X.t[:], func=AF.Square, accum_out=ssq.t[:, 0:1]),
                         reads=[X.buf], writes=[junk.buf, ssq.buf])
                S.op("act", lambda e: e.activation(out=ssq.t[:, 1:2], in_=ssq.t[:, 0:1], func=AF.Ln, scale=1.0 / D, bias=EPS),
                     reads=[ssq.buf], writes=[ssq.buf])
                S.op("act", lambda e: e.activation(out=ssq.t[:, 1:2], in_=ssq.t[:, 1:2], func=AF.Exp, scale=-0.5),
                     reads=[ssq.buf], writes=[ssq.buf])
                S.op("dve", lambda e: e.tensor_scalar(out=xb.t[:], in0=X.t[:], scalar1=ssq.t[:, 1:2], scalar2=None,
                                                      op0=ALU.mult), reads=[X.buf, ssq.buf], writes=[xb.buf])
                for c in range(8):
                    S.op("pe", lambda e: e.transpose(out=pT.t[:, c * 128:(c + 1) * 128], in_=xb.t[:, c * 128:(c + 1) * 128],
                                                     identity=IDB.ap), reads=[xb.buf, cbf.buf], writes=[pT.buf])
                S.op("dve", lambda e: e.tensor_copy(out=hTd.t[:, 0:4, tt * 128:(tt + 1) * 128], in_=hTd.t[:, 0:4, tt * 128:(tt + 1) * 128]),
                     reads=[], writes=[]) if False else None
                S.op("act", lambda e: e.copy(out=hTd.t[:, :, tt * 128:(tt + 1) * 128],
                                             in_=pT.t[:, :].rearrange("p (c t) -> p c t", c=8)),
                     reads=[pT.buf], writes=[hTd.buf])
                yield

        def x_to_hT(s):
            for _ in x_to_hT_gen(s):
                pass

        def proj_fm(col0, m, P, hTs=None):
            hTs = hTs or hT
            for c in range(8):
                S.op("pe", lambda e: e.matmul(P.t[0:m, :], lhsT=Wb.t[:, c, col0:col0 + m], rhs=hTs.t[:, c, :],
                                              start=(c == 0), stop=(c == 7)),
                     reads=[Wb.buf, hTs.buf], writes=[P.buf])

        def proj_fm_gen(col0, m, P, hTs, step=8):
            for c in range(8):
                S.op("pe", lambda e: e.matmul(P.t[0:m, :], lhsT=Wb.t[:, c, col0:col0 + m], rhs=hTs.t[:, c, :],
                                              start=(c == 0), stop=(c == 7)),
                     reads=[Wb.buf, hTs.buf], writes=[P.buf])
                if c % step == step - 1:
                    yield

        def run_rr(gens):
            active = list(gens)
            while active:
                for gen in list(active):
                    try:
                        next(gen)
                    except StopIteration:
                        active.remove(gen)

        with ExitStack() as stA:
            sa = lambda name, shape, dt: sb(name, shape, dt, stack=stA)
            KA = [sa("KA%d" % h, [96, S_LOC], BF16) for h in range(4)]
            KAb = [[Buf("KA%d_%d" % (h, s)) for s in range(n_super)] for h in range(4)]
            KAone = Buf("KAone")
            VA = sa("VA", [128, S_LOC // 128, 4, 65], BF16)
            VAb = [Buf("VA_%d" % s) for s in range(n_super)]
            KS = [sa("KS%d" % h, [64, 32], BF16) for h in range(4)]
            KSf = sa("KSf", [64, 2], F32)
            KSf2 = [KSf, sa("KSf_b", [64, 2], F32)]
            NBd = [sa("NBd%d" % h, [128, 128], BF16) for h in range(4)]
            NBo = [sa("NBo%d" % h, [128, 128], BF16) for h in range(4)]
            wq = sa("wq", [64, 2], F32)
            qraw2 = [qraw, sa("qraw_b", [64, SUP], F32)]
            rinv2 = [rinv, sa("rinv_b", [64, SUP], F32)]
            QA2 = [[sa("QA%d_%d" % (h, i), [96, SUP], BF16) for h in range(4)] for i in range(2)]
            SZA2 = [sa("SZA_%d" % i, [64, 4, SUP], BF16) for i in range(2)]
            hT2 = [hT, sa("hT_b", [128, 8, SUP], BF16)]
            SBK = [pS[0], pS[1], pD[0], pD[1]]
            PT = [sa("PT%d" % i, [128, SUP], BF16) for i in range(4)]
            OS = sa("OS", [65, SUP], F32)
            RD = sa("RD", [65, SUP], F32)
            RDh = sa("RDh", [65, 2 * SUP], BF16)
            G = [sa("G%d" % i, [128, 4, 32], F32) for i in range(2)]
            M8 = sa("M8", [128, 4, 8], F32)
            MBQ = [sa("MBQ%d" % i, [128, 96], BF16) for i in range(4)]
            YA = sa("YA", [64, 4, SUP], BF16)
            relb = sa("relb", [33, 4], F32)
            selt = Tile(qraw2[1].t, "selt_alias"); selt.buf = qraw2[1].buf
            fsb = Tile(rinv2[1].t, "fsb_alias"); fsb.buf = rinv2[1].buf
            hank = Tile(rinv.t, "hank_alias"); hank.buf = rinv.buf
            d_oh = S.dma_sem("oh")
            for h in range(4):
                dma(KA[h].t[64:96, :], oneh_d[:, 0:S_LOC], d_oh, writes=[KAone])
            S.op("dve", lambda e: e.memset(VA.t[:, :, :, 64:65], 1.0), writes=VAb)

            def attn_pass(hh):
                use_cache = hcache["have"]
                hcache["have"] = True
                load_weights(hh, 1024)
                S.op("dve", lambda e: e.tensor_copy(out=wq.t[:, 0:1], in_=vecs.t[0:64, 8:9]), reads=[vecs.buf], writes=[wq.buf])
                S.op("dve", lambda e: e.tensor_scalar(out=wq.t[:, 1:2], in0=vecs.t[0:64, 9:10], scalar1=8.0, scalar2=None,
                                                      op0=ALU.mult), reads=[vecs.buf], writes=[wq.buf])
                S.op("dve", lambda e: e.memset(relb.t[32:33, :], 1.0), writes=[relb.buf])
                dma(relb.t[0:32, :], relb_d[hh, :, :], d_t5, writes=[relb.buf])
                dma(selt.t[0:33, 0:384], sel_d[:, :], d_t5b, writes=[selt.buf])
                S.op("pe", lambda e: e.matmul(pA[0].t[0:4, 0:384], lhsT=relb.t[:, :], rhs=selt.t[0:33, 0:384], start=True, stop=True),
                     reads=[relb.buf, selt.buf], writes=[pA[0].buf])
                S.op("dve", lambda e: e.tensor_copy(out=fsb.t[0:4, 0:384], in_=pA[0].t[0:4, 0:384]), reads=[pA[0].buf], writes=[fsb.buf])
                dma(fscr_d[:, :], fsb.t[0:4, 0:384], d_fs, reads=[fsb.buf], writes=[fscr])
                for h in range(4):
                    for var, off, dst in ((0, 0, NBd), (1, 128, NBo)):
                        src = bass.AP(tensor=fscr_d.tensor, offset=h * 384 + off, ap=[[1, 128], [1, 128]])
                        dma(hank.t[:, 0:128], src, d_hk, reads=[fscr], writes=[hank.buf])
                        S.op("pe", lambda e: e.matmul(pA[1].t[:, 0:128], lhsT=JF.ap, rhs=hank.t[:, 0:128], start=True, stop=True),
                             reads=[cf32.buf, hank.buf], writes=[pA[1].buf])
                        S.op("dve", lambda e: e.tensor_copy(out=dst[h].t[:], in_=pA[1].t[:, 0:128]),
                             reads=[pA[1].buf], writes=[dst[h].buf])


                for h in range(4):
                    S.op("dve", lambda e: e.memset(KS[h].t[:], 0.0), writes=[KS[h].buf])
                    for i in range(2):
                        S.op("dve", lambda e: e.memset(QA2[i][h].t[64:96, :], 0.0), writes=[QA2[i][h].buf])
                for i in range(2):
                    S.op("dve", lambda e: e.memset(G[i].t[:], -1e30), writes=[G[i].buf])
                for i in range(4):
                    S.op("dve", lambda e: e.memset(MBQ[i].t[:], 0.0), writes=[MBQ[i].buf])

                def prod(s):
                    t0 = s * SUP
                    hTd, QAs, SZAs = hT2[s % 2], QA2[s % 2], SZA2[s % 2]
                    if use_cache:
                        hT_load(s, hTd, s % 2)
                        yield
                    for tt in (range(4) if not use_cache else ()):
                        r0 = t0 + tt * 128
                        xi = xk[0] % 2
                        xk[0] += 1
                        X = xt[xi]
                        dma(X.t[:], x_d[r0:r0 + 128, :], d_xt[xi], writes=[X.buf])
                        yield
                        S.op("dve", lambda e: e.scalar_tensor_tensor(out=junk.t[:], in0=X.t[:], scalar=1.0, in1=X.t[:], op0=ALU.mult, op1=ALU.mult,
                                                                     accum_out=ssq.t[:, 0:1]),
                             reads=[X.buf], writes=[junk.buf, ssq.buf])
                        yield
                        S.op("act", lambda e: e.activation(out=ssq.t[:, 1:2], in_=ssq.t[:, 0:1], func=AF.Ln, scale=1.0 / D, bias=EPS),
                             reads=[ssq.buf], writes=[ssq.buf])
                        yield
                        S.op("act", lambda e: e.activation(out=ssq.t[:, 1:2], in_=ssq.t[:, 1:2], func=AF.Exp, scale=-0.5),
                             reads=[ssq.buf], writes=[ssq.buf])
                        yield
                        S.op("dve", lambda e: e.tensor_scalar(out=xb.t[:], in0=X.t[:], scalar1=ssq.t[:, 1:2], scalar2=None,
                                                              op0=ALU.mult), reads=[X.buf, ssq.buf], writes=[xb.buf])
                        yield
                        for c in range(8):
                            S.op("pe", lambda e: e.transpose(out=pT.t[:, c * 128:(c + 1) * 128], in_=xb.t[:, c * 128:(c + 1) * 128],
                                                             identity=IDB.ap), reads=[xb.buf, cbf.buf], writes=[pT.buf])
                            if c == 3:
                                yield
                        yield
                        S.op("dve", lambda e: e.tensor_copy(out=hTd.t[:, :, tt * 128:(tt + 1) * 128],
                                                            in_=pT.t[:, :].rearrange("p (c t) -> p c t", c=8)),
                             reads=[pT.buf], writes=[hTd.buf])
                        yield
                    if not use_cache:
                        hT_store(s, hTd, s % 2)
                    if s == 0 and hh == 0:
                        dump("hT", hTd.t[:, :, :], [hTd.buf])
                    def qk_chain(ci_):
                        P = pA[ci_]
                        sq, qr, ri = sqt[ci_], qraw2[ci_], rinv2[ci_]
                        for gi in range(ci_, 8, 2):
                            which, h = gi // 4, gi % 4
                            yield from proj_fm_gen((QA0 if which == 0 else KA0) + h * 64, 64, P, hTd)
                            S.op("act", lambda e: e.activation(out=sq.t[0:64, :], in_=P.t[0:64, :], func=AF.Square),
                                 reads=[P.buf], writes=[sq.buf])
                            yield
                            S.op("dve", lambda e: e.tensor_copy(out=qr.t[0:64, :], in_=P.t[0:64, :]), reads=[P.buf], writes=[qr.buf])
                            S.op("pe", lambda e: e.matmul(P.t[0:64, :], lhsT=onesb.t[0:64, 0:64], rhs=sq.t[0:64, :], start=True, stop=True),
                                 reads=[onesb.buf, sq.buf], writes=[P.buf])
                            yield
                            S.op("act", lambda e: e.activation(out=ri.t[0:64, :], in_=P.t[0:64, :], func=AF.Ln, bias=64.0 * EPS),
                                 reads=[P.buf], writes=[ri.buf])
                            yield
                            S.op("act", lambda e: e.activation(out=ri.t[0:64, :], in_=ri.t[0:64, :], func=AF.Exp, scale=-0.5),
                                 reads=[ri.buf], writes=[ri.buf])
                            yield
                            if which == 0:
                                dst = QAs[h][0:64, :]
                            else:
                                dst = KA[h].v((slice(0, 64), slice(t0, t0 + SUP)), KAb[h][s])
                            S.op("dve", lambda e: e.scalar_tensor_tensor(out=dst.ap, in0=qr.t[0:64, :], scalar=wq.t[:, which:which + 1],
                                                                         in1=ri.t[0:64, :], op0=ALU.mult, op1=ALU.mult),
                                 reads=[qr.buf, wq.buf, ri.buf], writes=[dst.buf])
                            yield
                            if which == 1:
                                ksf = KSf2[ci_]
                                S.op("dve", lambda e: e.tensor_reduce(out=ksf.t[:, 0:2],
                                                                      in_=KA[h].t[0:64, t0:t0 + SUP].rearrange("p (b k) -> p b k", b=2),
                                                                      axis=AX.X, op=ALU.add),
                                     reads=[KAb[h][s]], writes=[ksf.buf])
                                yield
                                S.op("dve", lambda e: e.tensor_copy(out=KS[h].t[:, 2 * s:2 * s + 2], in_=ksf.t[:, 0:2]),
                                     reads=[ksf.buf], writes=[KS[h].buf])
                                yield

                    qk_active = [qk_chain(0), qk_chain(1)]
                    while qk_active:
                        for gen_ in list(qk_active):
                            try:
                                next(gen_)
                                yield
                            except StopIteration:
                                qk_active.remove(gen_)
                    if s == 0 and hh == 0:
                        for h in range(4):
                            dump("qa", QAs[h].t[0:64, :], [QAs[h].buf], (h,))
                            dump("ka", KA[h].t[0:64, 0:SUP], [KAb[h][0]], (h,))
                    for tt in range(4):
                        P = next_pA()
                        for c in range(8):
                            S.op("pe", lambda e: e.matmul(P.t[:, 0:256], lhsT=hTd.t[:, c, tt * 128:(tt + 1) * 128], rhs=Wb.t[:, c, VA0:VA0 + 256],
                                                          start=(c == 0), stop=(c == 7)), reads=[Wb.buf, hTd.buf], writes=[P.buf])
                            if c % 4 == 3:
                                yield
                        S.op("dve", lambda e: e.tensor_copy(out=VA.t[:, 4 * s + tt, :, 0:64], in_=P.t[:, 0:256].rearrange("p (h d) -> p h d", h=4)),
                             reads=[P.buf], writes=[VAb[s]])
                        yield
                    for h in range(4):
                        P = next_pA()
                        zt = rinv2[h % 2]
                        yield from proj_fm_gen(ZA0 + h * 64, 64, P, hTd)
                        S.op("act", lambda e: e.activation(out=zt.t[0:64, :], in_=P.t[0:64, :], func=AF.Exp, scale=-1.0), reads=[P.buf], writes=[zt.buf])
                        yield
                        S.op("act", lambda e: e.activation(out=zt.t[0:64, :], in_=zt.t[0:64, :], func=AF.Ln, bias=1.0), reads=[zt.buf], writes=[zt.buf])
                        yield
                        S.op("act", lambda e: e.activation(out=zt.t[0:64, :], in_=zt.t[0:64, :], func=AF.Exp, scale=-1.0), reads=[zt.buf], writes=[zt.buf])
                        yield
                        S.op("dve", lambda e: e.tensor_tensor(out=SZAs.t[:, h, :], in0=P.t[0:64, :], in1=zt.t[0:64, :], op=ALU.mult),
                             reads=[P.buf, zt.buf], writes=[SZAs.buf])
                        yield
                    for tt in range(4):
                        T = 4 * s + tt
                        ob = T // 2
                        if ob == 0:
                            continue
                        Gt = G[T % 2]
                        P = next_pA()
                        for h in range(4):
                            S.op("pe", lambda e: e.matmul(P.t[:, h * 32:h * 32 + ob], lhsT=QAs[h].t[0:64, tt * 128:(tt + 1) * 128],
                                                          rhs=KS[h].t[:, 0:ob], start=(h == 0), stop=(h == 3)),
                                 reads=[QAs[h].buf, KS[h].buf], writes=[P.buf])
                        yield
                        S.op("dve", lambda e: e.tensor_copy(out=Gt.t[:, :, 0:ob], in_=P.t[:, 0:128].rearrange("p (h n) -> p h n", h=4)[:, :, 0:ob]),
                             reads=[P.buf], writes=[Gt.buf])
                        yield
                        for h in range(4):
                            S.op("dve", lambda e: e.max(out=M8.t[:, h, :], in_=Gt.t[:, h, :]), reads=[Gt.buf], writes=[M8.buf])
                        yield
                        for h in range(4):
                            mq = MBQ[h]
                            S.op("dve", lambda e: e.tensor_scalar(out=mq.t[:, 64:64 + ob], in0=Gt.t[:, h, 0:ob], scalar1=M8.t[:, h, 2:3],
                                                                  scalar2=NEG, op0=ALU.is_lt, op1=ALU.mult),
                                 reads=[Gt.buf, M8.buf], writes=[mq.buf])
                        yield
                        for h in range(4):
                            mq = MBQ[h]
                            P2 = pA[h % 2]
                            S.op("pe", lambda e: e.matmul(P2.t[0:96, 0:128], lhsT=mq.t[:, :], rhs=IDB.ap, start=True, stop=True),
                                 reads=[mq.buf, cbf.buf], writes=[P2.buf])
                            yield
                            S.op("dve", lambda e: e.tensor_copy(out=QAs[h].t[64:96, tt * 128:(tt + 1) * 128], in_=P2.t[64:96, 0:128]),
                                 reads=[P2.buf], writes=[QAs[h].buf])
                            yield
                    if s == 1 and hh == 0:
                        for h in range(4):
                            dump("qam", QAs[h].t[64:96, :], [QAs[h].buf], (h,))

                def cons(s):
                    t0 = s * SUP
                    QAs, SZAs = QA2[s % 2], SZA2[s % 2]
                    nkt = 4 * (s + 1)
                    units = [(h, kt) for h in range(4) for kt in range(nkt)]
                    uinfo = {}

                    def att_qk(ui):
                        h, kt = units[ui]
                        j = kt - 4 * s
                        c0 = 0 if j <= 0 else j * 128
                        P = SBK[ui % 4]
                        near = []
                        if j >= 0:
                            near.append((NBd[h], j * 128))
                            if j < 3:
                                near.append((NBo[h], (j + 1) * 128))
                        elif j == -1:
                            near.append((NBo[h], 0))
                        S.op("pe", lambda e: e.matmul(P.t[:, c0:SUP], lhsT=KA[h].t[0:96, kt * 128:(kt + 1) * 128], rhs=QAs[h].t[0:96, c0:SUP],
                                                      start=True, stop=(len(near) == 0)),
                             reads=[KAb[h][kt // 4], KAone, QAs[h].buf], writes=[P.buf])
                        for ni, (nb, qc) in enumerate(near):
                            S.op("pe", lambda e: e.matmul(P.t[:, qc:qc + 128], lhsT=IDB.ap, rhs=nb.t[:, :], start=False,
                                                          stop=(ni == len(near) - 1)),
                                 reads=[cbf.buf, nb.buf], writes=[P.buf])
                        uinfo[ui] = (P, c0)

                    def att_exp_pv(ui):
                        h, kt = units[ui]
                        P, c0 = uinfo.pop(ui)
                        pt = PT[ui % len(PT)]
                        S.op("act", lambda e: e.activation(out=pt.t[:, c0:SUP], in_=P.t[:, c0:SUP], func=AF.Exp),
                             reads=[P.buf], writes=[pt.buf])
                        S.op("pe", lambda e: e.matmul(pO.t[0:65, c0:SUP], lhsT=VA.t[:, kt, h, :], rhs=pt.t[:, c0:SUP],
                                                      start=(kt == 0), stop=(kt == nkt - 1)),
                             reads=[VAb[kt // 4], pt.buf], writes=[pO.buf])

                    def att_tail_a(h):
                        S.op("act", lambda e: e.copy(out=OS.t[:, :], in_=pO.t[0:65, :]), reads=[pO.buf], writes=[OS.buf])
                        if s == 1 and hh == 0 and h == 0:
                            dump("os", OS.t[:, :], [OS.buf])
                        S.op("act", lambda e: e.activation(out=RD.t[64:65, :], in_=OS.t[64:65, :], func=AF.Ln), reads=[OS.buf], writes=[RD.buf])
                        S.op("act", lambda e: e.activation(out=RD.t[64:65, :], in_=RD.t[64:65, :], func=AF.Exp, scale=-1.0), reads=[RD.buf], writes=[RD.buf])
                        S.op("dve", lambda e: e.tensor_copy(out=RDh.t[64:65, 0:SUP], in_=RD.t[64:65, :]), reads=[RD.buf], writes=[RDh.buf])
                        S.op("dve", lambda e: e.tensor_tensor(out=RDh.t[64:65, SUP:2 * SUP], in0=RD.t[64:65, :], in1=RDh.t[64:65, 0:SUP], op=ALU.subtract),
                             reads=[RD.buf, RDh.buf], writes=[RDh.buf])

                    def att_tail_b(h, ui):
                        P = SBK[ui % 4]
                        S.op("pe", lambda e: e.matmul(P.t[0:64, :], lhsT=onesb.t[64:65, 0:64], rhs=RDh.t[64:65, 0:SUP], start=True, stop=False),
                             reads=[onesb.buf, RDh.buf], writes=[P.buf])
                        S.op("pe", lambda e: e.matmul(P.t[0:64, :], lhsT=onesb.t[64:65, 0:64], rhs=RDh.t[64:65, SUP:2 * SUP], start=False, stop=True),
                             reads=[onesb.buf, RDh.buf], writes=[P.buf])
                        S.op("dve", lambda e: e.tensor_tensor(out=OS.t[0:64, :], in0=OS.t[0:64, :], in1=P.t[0:64, :], op=ALU.mult),
                             reads=[OS.buf, P.buf], writes=[OS.buf])
                        S.op("dve", lambda e: e.tensor_tensor(out=YA.t[:, h, :], in0=OS.t[0:64, :], in1=SZAs.t[:, h, :], op=ALU.mult),
                             reads=[OS.buf, SZAs.buf], writes=[YA.buf])
                        if "oa" in dbg_d and hh == 0:
                            dump("oa", OS.t[0:64, :], [OS.buf], (h, slice(None), slice(t0, t0 + SUP)))

                    LOOK = 3
                    for u0 in range(min(LOOK, len(units))):
                        att_qk(u0)
                    pend_tail = None
                    for ui in range(len(units)):
                        h, kt = units[ui]
                        if ui + LOOK < len(units):
                            att_qk(ui + LOOK)
                        att_exp_pv(ui)
                        if pend_tail is not None and kt == 1:
                            att_tail_b(pend_tail, ui)
                            pend_tail = None
                        if kt == nkt - 1:
                            att_tail_a(h)
                            pend_tail = h
                        yield
                    att_tail_b(pend_tail, len(units) - 1)
                    dma(ys_d[256 * hh:256 * hh + 256, t0:t0 + SUP].rearrange("(h p) t -> p h t", h=4), YA.t[:, :, :], d_ys, reads=[YA.buf], writes=[ysb], eng="sp")


                n_run = n_super if not DBG_ONE else 1
                if SEQ_DEBUG:
                    for s in range(n_run):
                        run_rr([prod(s)])
                        run_rr([cons(s)])
                else:
                    run_rr([prod(0)])
                    for s in range(n_run):
                        run_rr([cons(s)] + ([prod(s + 1)] if s + 1 < n_run else []))

            for pname in passes:
                if pname[0] == "A":
                    attn_pass(int(pname[1]))
            S.barrier()
            S.barrier()

        with ExitStack() as stD:
            sd = lambda name, shape, dt: sb(name, shape, dt, stack=stD)
            negA = sd("negA", [128, 4], F32)
            SZD2 = [sd("SZD_%d" % i, [128, 2, SUP], BF16) for i in range(2)]
            CIN = [sd("CIN0", [128, SUP + 3], F32), sd("CIN1", [128, SUP + 3], F32)]
            CARRY = sd("CARRY", [128, 12, 3], F32)
            CVA = sd("CVA", [128, 4, SUP], F32)
            CVV = sd("CVV", [128, 2, SUP], F32)
            QT2 = [sd("QT_%d" % i, [128, 2, SUP], BF16) for i in range(2)]
            KT2 = [sd("KT_%d" % i, [128, 2, SUP], BF16) for i in range(2)]
            VCb2 = [sd("VCb_%d" % i, [128, 2, SUP], BF16) for i in range(2)]
            Ktok2 = [sd("Ktok_%d" % i, [128, 2, 4, 128], BF16) for i in range(2)]
            Vtok2 = [sd("Vtok_%d" % i, [128, 2, 4, 128], BF16) for i in range(2)]
            BAt = sd("BAt", [128, 4, 4], F32)
            beta2 = [sd("beta_%d" % i, [128, 4, 2], F32) for i in range(2)]
            gg2 = [sd("gg_%d" % i, [128, 4, 2], F32) for i in range(2)]
            YD = sd("YD", [128, 2, SUP], BF16)
            CH = []
            for ci in range(4):
                nm = lambda n: "%s_%d" % (n, ci)
                Tl = {
                    "banks": (pD[0], pD[1]) if ci % 2 == 0 else (pS[0], pS[1]),
                    "SC": sd(nm("SC"), [128, 6], F32), "bege": sd(nm("bege"), [128, 2], F32),
                    "GTri": sd(nm("GTri"), [128, 2, 128], F32), "DECs": sd(nm("DECs"), [128, 2, 128], F32),
                    "DECi": sd(nm("DECi"), [128, 2, 128], F32), "EGC": sd(nm("EGC"), [128, 2, 128], F32),
                    "Lm": sd(nm("Lm"), [128, 2, 128], BF16),
                    "Pm": [sd(nm("Pm0"), [128, 2, 128], BF16), sd(nm("Pm1"), [128, 2, 128], BF16)],
                    "Qm": [sd(nm("Qm0"), [128, 2, 128], BF16), sd(nm("Qm1"), [128, 2, 128], BF16)],
                    "Xf": sd(nm("Xf"), [128, 2, 128], F32), "Xb": sd(nm("Xb"), [128, 2, 128], BF16),
                    "Yb": sd(nm("Yb"), [128, 2, 128], BF16), "Mm": sd(nm("Mm"), [128, 2, 128], BF16),
                    "Tb": sd(nm("Tb"), [128, 2, 128], BF16), "AU": sd(nm("AU"), [128, 2, 128], BF16),
                    "AW": sd(nm("AW"), [128, 2, 128], BF16), "AT": sd(nm("AT"), [128, 2, 128], BF16),
                    "QDT": sd(nm("QDT"), [128, 2, 128], BF16), "U": sd(nm("U"), [128, 2, 128], F32),
                    "WT": sd(nm("WT"), [128, 2, 128], BF16),
                }
                CH.append(Tl)
            VNf = sd("VNf", [128, 2, 128], F32)
            VNb = sd("VNb", [128, 2, 128], BF16)
            VNSb = sd("VNSb", [128, 2, 128], BF16)
            Sf = sd("Sf", [128, 4, 128], F32)
            Sb_ = sd("Sb", [128, 4, 128], BF16)
            OT = sd("OT", [128, 2, SUP], F32)

            def dn_pass():
                use_cache = hcache["have"]
                hcache["have"] = True
                load_weights(2, NCOL)
                S.op("act", lambda e: e.activation(out=negA.t[:], in_=vecs.t[:, 56:60], func=AF.Exp),
                     reads=[vecs.buf], writes=[negA.buf])
                S.op("dve", lambda e: e.tensor_scalar(out=negA.t[:], in0=negA.t[:], scalar1=-1.0, scalar2=None, op0=ALU.mult),
                     reads=[negA.buf], writes=[negA.buf])
                S.op("dve", lambda e: e.memset(CARRY.t[:], 0.0), writes=[CARRY.buf])
                S.op("dve", lambda e: e.memset(Sf.t[:], 0.0), writes=[Sf.buf])
                S.op("dve", lambda e: e.memset(Sb_.t[:], 0.0), writes=[Sb_.buf])
                def rr_gen(gens):
                    active = list(gens)
                    while active:
                        for gen in list(active):
                            try:
                                next(gen)
                                yield
                            except StopIteration:
                                active.remove(gen)

                def dn_prep(s, g):
                    t0 = s * SUP
                    par = (2 * s + g) % 2
                    QT, KT, VCb, Ktok, Vtok, SZD, beta, gg = QT2[par], KT2[par], VCb2[par], Ktok2[par], Vtok2[par], SZD2[par], beta2[par], gg2[par]
                    if g == 0:
                        if use_cache:
                            hT_load(s, hT, 0)
                            yield
                        else:
                            yield from x_to_hT_gen(s)
                            hT_store(s, hT, 0)
                    gc0 = GCOL * g
                    P = next_pA()
                    for tt in range(4):
                        for c in range(8):
                            S.op("pe", lambda e: e.matmul(P.t[:, tt * 4:tt * 4 + 4], lhsT=hT.t[:, c, tt * 128:(tt + 1) * 128],
                                                          rhs=Wb.t[:, c, gc0 + BA0:gc0 + BA0 + 4], start=(c == 0 and tt == 0), stop=(c == 7)),
                                 reads=[Wb.buf, hT.buf], writes=[P.buf])
                    S.op("dve", lambda e: e.tensor_copy(out=BAt.t[:, :, :], in_=P.t[:, 0:16].rearrange("p (t f) -> p t f", t=4)),
                         reads=[P.buf], writes=[BAt.buf])
                    S.op("act", lambda e: e.activation(out=beta.t[:], in_=BAt.t[:, :, 0:2], func=AF.Exp, scale=-1.0),
                         reads=[BAt.buf], writes=[beta.buf])
                    S.op("dve", lambda e: e.tensor_scalar(out=beta.t[:], in0=beta.t[:], scalar1=1.0, scalar2=None, op0=ALU.add),
                         reads=[beta.buf], writes=[beta.buf])
                    S.op("dve", lambda e: e.reciprocal(out=beta.t[:], in_=beta.t[:]), reads=[beta.buf], writes=[beta.buf])
                    for tt in range(4):
                        S.op("dve", lambda e: e.tensor_tensor(out=gg.t[:, tt, :], in0=BAt.t[:, tt, 2:4], in1=vecs.t[:, 60 + 2 * g:62 + 2 * g], op=ALU.add),
                             reads=[BAt.buf, vecs.buf], writes=[gg.buf])
                    S.op("act", lambda e: e.activation(out=gg.t[:], in_=gg.t[:], func=AF.Exp), reads=[gg.buf], writes=[gg.buf])
                    S.op("act", lambda e: e.activation(out=gg.t[:], in_=gg.t[:], func=AF.Ln, bias=1.0), reads=[gg.buf], writes=[gg.buf])
                    for tt in range(4):
                        S.op("dve", lambda e: e.tensor_tensor(out=gg.t[:, tt, :], in0=gg.t[:, tt, :], in1=negA.t[:, 2 * g:2 * g + 2], op=ALU.mult),
                             reads=[gg.buf, negA.buf], writes=[gg.buf])
                    if s == 0 and g == 0:
                        dump("beta", beta.t[:], [beta.buf])
                        dump("gg", gg.t[:], [gg.buf])
                    yield
                    for j in range(2):
                        P = next_pA()
                        proj_fm(gc0 + ZD0 + j * 128, 128, P)
                        S.op("act", lambda e: e.copy(out=SZD.t[:, j, :], in_=P.t[:, :]), reads=[P.buf], writes=[SZD.buf])
                        yield
                    for cc in range(6):
                        P = next_pA()
                        proj_fm(gc0 + QD0 + cc * 128, 128, P)
                        ci = CIN[cc % 2]
                        vc = 8 + 24 * g + cc * 4
                        S.op("act", lambda e: e.copy(out=ci.t[:, 3:SUP + 3], in_=P.t[:, :]), reads=[P.buf], writes=[ci.buf])
                        S.op("act", lambda e: e.copy(out=ci.t[:, 0:3], in_=CARRY.t[:, 6 * g + cc, :]), reads=[CARRY.buf], writes=[ci.buf])
                        S.op("act", lambda e: e.copy(out=CARRY.t[:, 6 * g + cc, :], in_=ci.t[:, SUP:SUP + 3]), reads=[ci.buf], writes=[CARRY.buf])
                        cvo = CVA[:, cc, :] if cc < 4 else CVV[:, cc - 4, :]
                        S.op("dve", lambda e: e.tensor_scalar(out=cvo.ap, in0=ci.t[:, 0:SUP], scalar1=vecs.t[:, vc:vc + 1],
                                                              scalar2=None, op0=ALU.mult), reads=[ci.buf, vecs.buf], writes=[cvo.buf])
                        for kk in range(1, 4):
                            S.op("dve", lambda e: e.scalar_tensor_tensor(out=cvo.ap, in0=ci.t[:, kk:kk + SUP],
                                                                         scalar=vecs.t[:, vc + kk:vc + kk + 1],
                                                                         in1=cvo.ap, op0=ALU.mult, op1=ALU.add),
                                 reads=[ci.buf, vecs.buf, cvo.buf], writes=[cvo.buf])
                        yield
                    for j in range(2):
                        S.op("act", lambda e: e.activation(out=SZD.t[:, j, :], in_=SZD.t[:, j, :], func=AF.Silu), reads=[SZD.buf], writes=[SZD.buf])
                    for cc in range(4):
                        S.op("act", lambda e: e.activation(out=CVA.t[:, cc, :], in_=CVA.t[:, cc, :], func=AF.Silu), reads=[CVA.buf], writes=[CVA.buf])
                    for j in range(2):
                        S.op("act", lambda e: e.activation(out=VCb.t[:, j, :], in_=CVV.t[:, j, :], func=AF.Silu), reads=[CVV.buf], writes=[VCb.buf])
                    yield
                    if s == 0 and g == 0:
                        for cc in range(4):
                            dump("cv", CVA.t[:, cc, :], [CVA.buf], (cc,))
                    for cc in range(4):
                        j = cc % 2
                        sq = sqt[cc % 2]
                        S.op("act", lambda e: e.activation(out=sq.t[:], in_=CVA.t[:, cc, :], func=AF.Square), reads=[CVA.buf], writes=[sq.buf])
                        P2 = next_pA()
                        S.op("pe", lambda e: e.matmul(P2.t[:, :], lhsT=onesb.t[:, :], rhs=sq.t[:, :], start=True, stop=True),
                             reads=[onesb.buf, sq.buf], writes=[P2.buf])
                        S.op("act", lambda e: e.activation(out=rinv.t[:], in_=P2.t[:, :], func=AF.Ln, bias=EPS), reads=[P2.buf], writes=[rinv.buf])
                        S.op("act", lambda e: e.activation(out=rinv.t[:], in_=rinv.t[:], func=AF.Exp, scale=-0.5), reads=[rinv.buf], writes=[rinv.buf])
                        dstT = QT if cc < 2 else KT
                        scl = 128.0 ** -0.5 if cc < 2 else 1.0
                        S.op("dve", lambda e: e.scalar_tensor_tensor(out=dstT.t[:, j, :], in0=CVA.t[:, cc, :], scalar=scl, in1=rinv.t[:],
                                                                     op0=ALU.mult, op1=ALU.mult),
                             reads=[CVA.buf, rinv.buf], writes=[dstT.buf])
                        yield
                    if s == 0 and g == 0:
                        dump("qt", QT.t[:], [QT.buf])
                        dump("kt", KT.t[:], [KT.buf])
                    for j in range(2):
                        for c in range(4):
                            S.op("pe", lambda e: e.transpose(out=pT.t[:, c * 128:(c + 1) * 128], in_=KT.t[:, j, c * 128:(c + 1) * 128], identity=IDB.ap),
                                 reads=[KT.buf, cbf.buf], writes=[pT.buf])
                            S.op("pe", lambda e: e.transpose(out=pT.t[:, 512 + c * 128:512 + (c + 1) * 128], in_=VCb.t[:, j, c * 128:(c + 1) * 128],
                                                             identity=IDB.ap), reads=[VCb.buf, cbf.buf], writes=[pT.buf])
                        S.op("act", lambda e: e.copy(out=Ktok.t[:, j, :, :], in_=pT.t[:, 0:512].rearrange("p (c t) -> p c t", c=4)),
                             reads=[pT.buf], writes=[Ktok.buf])
                        S.op("dve", lambda e: e.tensor_copy(out=Vtok.t[:, j, :, :], in_=pT.t[:, 512:1024].rearrange("p (c t) -> p c t", c=4)),
                             reads=[pT.buf], writes=[Vtok.buf])
                        yield


                def dn_group(s, g):
                    t0 = s * SUP
                    par = (2 * s + g) % 2
                    QT, KT, VCb, Ktok, Vtok, SZD, beta, gg = QT2[par], KT2[par], VCb2[par], Ktok2[par], Vtok2[par], SZD2[par], beta2[par], gg2[par]
                    def v2(T, a):
                        return T.t[:, a:a + 256].rearrange("p (j i) -> p j i", j=2)

                    def dn_phase1(c, Tl):
                        cs = slice(c * 128, (c + 1) * 128)
                        D0, D1 = Tl["banks"]
                        SC, bege, GTri, DECs, DECi, EGC, Lm, Pm, Qm, Xf, Xb, Yb, Mm, Tb, AU, AW, AT, QDT, U, WT = [Tl[k] for k in (
                            "SC", "bege", "GTri", "DECs", "DECi", "EGC", "Lm", "Pm", "Qm", "Xf", "Xb", "Yb", "Mm", "Tb", "AU", "AW", "AT", "QDT", "U", "WT")]
                        for i3, lhs in enumerate((TRIU, OMT, onesf[:, :])):
                            S.op("pe", lambda e: e.matmul(D0.t[:, 384 + 2 * i3:386 + 2 * i3], lhsT=lhs.ap, rhs=gg.t[:, c, :], start=(i3 == 0), stop=(i3 == 2)),
                                 reads=[lhs.buf, gg.buf], writes=[D0.buf])
                        S.op("act", lambda e: e.activation(out=SC.t[:], in_=D0.t[:, 384:390], func=AF.Exp), reads=[D0.buf], writes=[SC.buf])
                        S.op("dve", lambda e: e.tensor_tensor(out=bege.t[:], in0=beta.t[:, c, :], in1=SC.t[:, 0:2], op=ALU.mult),
                             reads=[beta.buf, SC.buf], writes=[bege.buf])
                        for j in range(2):
                            S.op("dve", lambda e: e.tensor_scalar(out=GTri.t[:, j, :], in0=TRIU.ap, scalar1=gg.t[:, c, j:j + 1], scalar2=None, op0=ALU.mult),
                                 reads=[cf32.buf, gg.buf], writes=[GTri.buf])
                        yield
                        for j in range(2):
                            o = j * 128
                            S.op("pe", lambda e: e.matmul(D0.t[:, o:o + 128], lhsT=GTri.t[:, j, :], rhs=onesf.t[:, :], start=(j == 0), stop=False),
                                 reads=[GTri.buf, onesf.buf], writes=[D0.buf])
                            S.op("pe", lambda e: e.matmul(D0.t[:, o:o + 128], lhsT=negonesf.t[:, :], rhs=GTri.t[:, j, :], start=False, stop=False),
                                 reads=[GTri.buf, negonesf.buf], writes=[D0.buf])
                            S.op("pe", lambda e: e.matmul(D0.t[:, o:o + 128], lhsT=IDB.ap, rhs=MBS.ap, start=False, stop=(j == 1)),
                                 reads=[cbf.buf], writes=[D0.buf])
                        for j in range(2):
                            o = j * 128
                            S.op("pe", lambda e: e.matmul(D1.t[:, o:o + 128], lhsT=onesf.t[:, :], rhs=GTri.t[:, j, :], start=(j == 0), stop=False),
                                 reads=[GTri.buf, onesf.buf], writes=[D1.buf])
                            S.op("pe", lambda e: e.matmul(D1.t[:, o:o + 128], lhsT=GTri.t[:, j, :], rhs=negonesf.t[:, :], start=False, stop=False),
                                 reads=[GTri.buf, negonesf.buf], writes=[D1.buf])
                            S.op("pe", lambda e: e.matmul(D1.t[:, o:o + 128], lhsT=IDB.ap, rhs=MBI.ap, start=False, stop=False),
                                 reads=[cbf.buf], writes=[D1.buf])
                            S.op("pe", lambda e: e.matmul(D1.t[:, 256 + o:256 + o + 128], lhsT=onesf.t[:, :], rhs=GTri.t[:, j, :], start=False, stop=(j == 1)),
                                 reads=[GTri.buf, onesf.buf], writes=[D1.buf])
                        S.op("act", lambda e: e.activation(out=DECs.t[:, :, :], in_=v2(D0, 0), func=AF.Exp), reads=[D0.buf], writes=[DECs.buf])
                        S.op("act", lambda e: e.activation(out=DECi.t[:, :, :], in_=v2(D1, 0), func=AF.Exp), reads=[D1.buf], writes=[DECi.buf])
                        S.op("act", lambda e: e.activation(out=EGC.t[:, :, :], in_=v2(D1, 256), func=AF.Exp), reads=[D1.buf], writes=[EGC.buf])
                        yield
                        for j in range(2):
                            o = j * 128
                            S.op("pe", lambda e: e.matmul(D0.t[:, o:o + 128], lhsT=KT.t[:, j, cs], rhs=KT.t[:, j, cs], start=(j == 0), stop=(j == 1)),
                                 reads=[KT.buf], writes=[D0.buf])
                        for j in range(2):
                            o = j * 128
                            S.op("pe", lambda e: e.matmul(D1.t[:, o:o + 128], lhsT=KT.t[:, j, cs], rhs=QT.t[:, j, cs], start=(j == 0), stop=(j == 1)),
                                 reads=[KT.buf, QT.buf], writes=[D1.buf])
                        for j in range(2):
                            o = j * 128
                            S.op("dve", lambda e: e.scalar_tensor_tensor(out=Lm.t[:, j, :], in0=D0.t[:, o:o + 128], scalar=beta.t[:, c, j:j + 1],
                                                                         in1=DECs.t[:, j, :], op0=ALU.mult, op1=ALU.mult),
                                 reads=[D0.buf, beta.buf, DECs.buf], writes=[Lm.buf])
                        S.op("dve", lambda e: e.tensor_tensor(out=AT.t[:, :, :], in0=v2(D1, 0), in1=DECi.t[:, :, :], op=ALU.mult),
                             reads=[D1.buf, DECi.buf], writes=[AT.buf])
                        S.op("dve", lambda e: e.tensor_tensor(out=QDT.t[:, :, :], in0=QT.t[:, :, cs], in1=EGC.t[:, :, :], op=ALU.mult),
                             reads=[QT.buf, EGC.buf], writes=[QDT.buf])
                        for j in range(2):
                            S.op("dve", lambda e: e.tensor_tensor(out=Pm[0].t[:, j, :], in0=Lm.t[:, j, :], in1=M32.ap, op=ALU.mult),
                                 reads=[Lm.buf, cbf.buf], writes=[Pm[0].buf])
                        yield
                        for j in range(2):
                            S.op("pe", lambda e: e.transpose(out=pT.t[:, j * 128:(j + 1) * 128], in_=Lm.t[:, j, :], identity=IDB.ap),
                                 reads=[Lm.buf, cbf.buf], writes=[pT.buf])
                        for j in range(2):
                            S.op("pe", lambda e: e.transpose(out=pT.t[:, 256 + j * 128:256 + (j + 1) * 128], in_=Pm[0].t[:, j, :], identity=IDB.ap),
                                 reads=[Pm[0].buf, cbf.buf], writes=[pT.buf])
                        S.op("act", lambda e: e.copy(out=Mm.t[:, :, :], in_=pT.t[:, 0:256].rearrange("p (j i) -> p j i", j=2)), reads=[pT.buf], writes=[Mm.buf])
                        S.op("act", lambda e: e.copy(out=Qm[0].t[:, :, :], in_=pT.t[:, 256:512].rearrange("p (j i) -> p j i", j=2)), reads=[pT.buf], writes=[Qm[0].buf])
                        for j in range(2):
                            S.op("dve", lambda e: e.tensor_tensor(out=Xb.t[:, j, :], in0=IDB.ap, in1=pT.t[:, 256 + j * 128:256 + (j + 1) * 128], op=ALU.subtract),
                                 reads=[cbf.buf, pT.buf], writes=[Xb.buf])
                            S.op("dve", lambda e: e.tensor_tensor(out=Yb.t[:, j, :], in0=IDB.ap, in1=Pm[0].t[:, j, :], op=ALU.subtract),
                                 reads=[cbf.buf, Pm[0].buf], writes=[Yb.buf])
                        yield
                        Pc, Qc = Pm[0], Qm[0]
                        for kstep in range(1, 6):
                            Pn, Qn = Pm[kstep % 2], Qm[kstep % 2]
                            if kstep >= 2:
                                for j in range(2):
                                    o = j * 128
                                    S.op("pe", lambda e: e.matmul(D1.t[:, o:o + 128], lhsT=Pc.t[:, j, :], rhs=Xb.t[:, j, :], start=(j == 0), stop=False),
                                         reads=[Pc.buf, Xb.buf], writes=[D1.buf])
                                    S.op("pe", lambda e: e.matmul(D1.t[:, 256 + o:256 + o + 128], lhsT=Qc.t[:, j, :], rhs=Yb.t[:, j, :], start=False, stop=(j == 1)),
                                         reads=[Qc.buf, Yb.buf], writes=[D1.buf])
                            if kstep <= 4:
                                for j in range(2):
                                    o = j * 128
                                    S.op("pe", lambda e: e.matmul(D0.t[:, o:o + 128], lhsT=Qc.t[:, j, :], rhs=Pc.t[:, j, :], start=(j == 0), stop=False),
                                         reads=[Qc.buf, Pc.buf], writes=[D0.buf])
                                    S.op("pe", lambda e: e.matmul(D0.t[:, 256 + o:256 + o + 128], lhsT=Pc.t[:, j, :], rhs=Qc.t[:, j, :], start=False, stop=(j == 1)),
                                         reads=[Qc.buf, Pc.buf], writes=[D0.buf])
                            if kstep >= 2:
                                S.op("dve", lambda e: e.tensor_tensor(out=Xb.t[:, :, :], in0=v2(D1, 0), in1=Xb.t[:, :, :], op=ALU.add),
                                     reads=[D1.buf, Xb.buf], writes=[Xb.buf])
                                S.op("dve", lambda e: e.tensor_tensor(out=Yb.t[:, :, :], in0=v2(D1, 256), in1=Yb.t[:, :, :], op=ALU.add),
                                     reads=[D1.buf, Yb.buf], writes=[Yb.buf])
                            if kstep <= 4:
                                S.op("act", lambda e: e.copy(out=Pn.t[:, :, :], in_=v2(D0, 0)), reads=[D0.buf], writes=[Pn.buf])
                                S.op("act", lambda e: e.copy(out=Qn.t[:, :, :], in_=v2(D0, 256)), reads=[D0.buf], writes=[Qn.buf])
                                Pc, Qc = Pn, Qn
                            yield
                        for lvl, MB in ((1, MB1), (2, MB2)):
                            for j in range(2):
                                o = j * 128
                                S.op("pe", lambda e: e.matmul(D0.t[:, o:o + 128], lhsT=Mm.t[:, j, :], rhs=Yb.t[:, j, :], start=(j == 0), stop=(j == 1)),
                                     reads=[Mm.buf, Yb.buf], writes=[D0.buf])
                            for j in range(2):
                                o = j * 128
                                S.op("dve", lambda e: e.tensor_tensor(out=Tb.t[:, j, :], in0=D0.t[:, o:o + 128], in1=MB.ap, op=ALU.mult),
                                     reads=[D0.buf, cbf.buf], writes=[Tb.buf])
                            yield
                            for j in range(2):
                                o = j * 128
                                S.op("pe", lambda e: e.matmul(D1.t[:, o:o + 128], lhsT=Tb.t[:, j, :], rhs=Xb.t[:, j, :], start=(j == 0), stop=(lvl == 2 and j == 1)),
                                     reads=[Tb.buf, Xb.buf], writes=[D1.buf])
                                if lvl == 1:
                                    S.op("pe", lambda e: e.matmul(D1.t[:, 256 + o:256 + o + 128], lhsT=Xb.t[:, j, :], rhs=Tb.t[:, j, :], start=False, stop=(j == 1)),
                                         reads=[Tb.buf, Xb.buf], writes=[D1.buf])
                            if lvl == 1:
                                S.op("dve", lambda e: e.tensor_tensor(out=Xb.t[:, :, :], in0=Xb.t[:, :, :], in1=v2(D1, 0), op=ALU.subtract),
                                     reads=[D1.buf, Xb.buf], writes=[Xb.buf])
                                S.op("dve", lambda e: e.tensor_tensor(out=Yb.t[:, :, :], in0=Yb.t[:, :, :], in1=v2(D1, 256), op=ALU.subtract),
                                     reads=[D1.buf, Yb.buf], writes=[Yb.buf])
                            else:
                                S.op("dve", lambda e: e.tensor_tensor(out=Xf.t[:, :, :], in0=Xb.t[:, :, :], in1=v2(D1, 0), op=ALU.subtract),
                                     reads=[D1.buf, Xb.buf], writes=[Xf.buf])
                                for j in range(2):
                                    S.op("dve", lambda e: e.tensor_scalar(out=AU.t[:, j, :], in0=Xf.t[:, j, :], scalar1=beta.t[:, c, j:j + 1], scalar2=None, op0=ALU.mult),
                                         reads=[Xf.buf, beta.buf], writes=[AU.buf])
                                    S.op("dve", lambda e: e.tensor_scalar(out=AW.t[:, j, :], in0=Xf.t[:, j, :], scalar1=bege.t[:, j:j + 1], scalar2=None, op0=ALU.mult),
                                         reads=[Xf.buf, bege.buf], writes=[AW.buf])
                            yield
                        for j in range(2):
                            o = j * 128
                            S.op("pe", lambda e: e.matmul(D0.t[:, o:o + 128], lhsT=AU.t[:, j, :], rhs=Vtok.t[:, j, c, :], start=(j == 0), stop=False),
                                 reads=[AU.buf, Vtok.buf], writes=[D0.buf])
                            S.op("pe", lambda e: e.matmul(D0.t[:, 256 + o:256 + o + 128], lhsT=Ktok.t[:, j, c, :], rhs=AW.t[:, j, :], start=False, stop=(j == 1)),
                                 reads=[AW.buf, Ktok.buf], writes=[D0.buf])
                        S.op("act", lambda e: e.copy(out=U.t[:, :, :], in_=v2(D0, 0)), reads=[D0.buf], writes=[U.buf])
                        S.op("dve", lambda e: e.tensor_copy(out=WT.t[:, :, :], in_=v2(D0, 256)), reads=[D0.buf], writes=[WT.buf])
                        yield

                    def dn_phase2(c, Tl):
                        cs = slice(c * 128, (c + 1) * 128)
                        SC, AT, QDT, U, WT = Tl["SC"], Tl["AT"], Tl["QDT"], Tl["U"], Tl["WT"]
                        R = pO
                        for j in range(2):
                            o = j * 128
                            S.op("pe", lambda e: e.matmul(R.t[:, o:o + 128], lhsT=WT.t[:, j, :], rhs=Sb_.t[:, 2 * g + j, :], start=(j == 0), stop=(j == 1)),
                                 reads=[WT.buf, Sb_.buf], writes=[R.buf])
                        S.op("dve", lambda e: e.tensor_tensor(out=VNf.t[:, :, :], in0=U.t[:, :, :], in1=v2(R, 0), op=ALU.subtract),
                             reads=[U.buf, R.buf], writes=[VNf.buf])
                        S.op("act", lambda e: e.copy(out=VNb.t[:, :, :], in_=VNf.t[:, :, :]), reads=[VNf.buf], writes=[VNb.buf])
                        for j in range(2):
                            S.op("dve", lambda e: e.tensor_scalar(out=VNSb.t[:, j, :], in0=VNf.t[:, j, :], scalar1=SC.t[:, 2 + j:3 + j], scalar2=None, op0=ALU.mult),
                                 reads=[VNf.buf, SC.buf], writes=[VNSb.buf])
                        yield
                        for j in range(2):
                            o = j * 128
                            S.op("pe", lambda e: e.matmul(R.t[:, 256 + o:256 + o + 128], lhsT=Sb_.t[:, 2 * g + j, :], rhs=QDT.t[:, j, :], start=(j == 0), stop=False),
                                 reads=[Sb_.buf, QDT.buf], writes=[R.buf])
                            S.op("pe", lambda e: e.matmul(R.t[:, 256 + o:256 + o + 128], lhsT=VNb.t[:, j, :], rhs=AT.t[:, j, :], start=False, stop=False),
                                 reads=[VNb.buf, AT.buf], writes=[R.buf])
                        for j in range(2):
                            o = j * 128
                            S.op("pe", lambda e: e.matmul(R.t[:, o:o + 128], lhsT=Ktok.t[:, j, c, :], rhs=VNSb.t[:, j, :], start=False, stop=(j == 1)),
                                 reads=[Ktok.buf, VNSb.buf], writes=[R.buf])
                        S.op("act", lambda e: e.copy(out=OT.t[:, :, cs], in_=v2(R, 256)), reads=[R.buf], writes=[OT.buf])
                        for j in range(2):
                            o = j * 128
                            S.op("dve", lambda e: e.scalar_tensor_tensor(out=Sf.t[:, 2 * g + j, :], in0=Sf.t[:, 2 * g + j, :], scalar=SC.t[:, 4 + j:5 + j], in1=R.t[:, o:o + 128],
                                                                         op0=ALU.mult, op1=ALU.add), reads=[Sf.buf, SC.buf, R.buf], writes=[Sf.buf])
                        S.op("act", lambda e: e.copy(out=Sb_.t[:, 2 * g:2 * g + 2, :], in_=Sf.t[:, 2 * g:2 * g + 2, :]), reads=[Sf.buf], writes=[Sb_.buf])
                        yield

                    def seq(*gens):
                        for gen in gens:
                            yield from gen

                    def run_rr(gens):
                        active = list(gens)
                        while active:
                            for gen in list(active):
                                try:
                                    next(gen)
                                except StopIteration:
                                    active.remove(gen)

                    yield from rr_gen([dn_phase1(cc_, CH[cc_]) for cc_ in range(4)])
                    for cc_ in range(4):
                        yield from dn_phase2(cc_, CH[cc_])
                    if "od" in dbg_d:
                        for j in range(2):
                            dump("od", OT.t[:, j, :], [OT.buf], (2 * g + j, slice(None), slice(t0, t0 + SUP)))
                    for j in range(2):
                        sq = sqt[j]
                        S.op("act", lambda e: e.activation(out=sq.t[:], in_=OT.t[:, j, :], func=AF.Square), reads=[OT.buf], writes=[sq.buf])
                        P2 = next_pA()
                        S.op("pe", lambda e: e.matmul(P2.t[:, :], lhsT=onesb.t[:, :], rhs=sq.t[:, :], start=True, stop=True),
                             reads=[onesb.buf, sq.buf], writes=[P2.buf])
                        S.op("act", lambda e: e.activation(out=rinv.t[:], in_=P2.t[:, :], func=AF.Ln, scale=1.0 / 128.0, bias=EPS),
                             reads=[P2.buf], writes=[rinv.buf])
                        S.op("act", lambda e: e.activation(out=rinv.t[:], in_=rinv.t[:], func=AF.Exp, scale=-0.5), reads=[rinv.buf], writes=[rinv.buf])
                        S.op("dve", lambda e: e.scalar_tensor_tensor(out=qraw.t[:], in0=OT.t[:, j, :], scalar=vecs.t[:, 64:65], in1=rinv.t[:],
                                                                     op0=ALU.mult, op1=ALU.mult), reads=[OT.buf, vecs.buf, rinv.buf], writes=[qraw.buf])
                        S.op("dve", lambda e: e.tensor_tensor(out=YD.t[:, j, :], in0=qraw.t[:], in1=SZD.t[:, j, :], op=ALU.mult),
                             reads=[qraw.buf, SZD.buf], writes=[YD.buf])
                        yield
                    dma(ys_d[512 + 256 * g:512 + 256 * g + 256, t0:t0 + SUP].rearrange("(h p) t -> p h t", h=2), YD.t[:, :, :], d_ys, reads=[YD.buf], writes=[ysb], eng="sp")


                groups = [(s_, g_) for s_ in range(n_super) for g_ in range(2)]
                run_rr([dn_prep(*groups[0])])
                for k_, (s_, g_) in enumerate(groups):
                    run_rr([dn_group(s_, g_)] + ([dn_prep(*groups[k_ + 1])] if k_ + 1 < len(groups) else []))

            if "D" in passes:
                dn_pass()
            S.barrier()

        if "ys" in dbg_d:
            with ExitStack() as stY:
                d_y2 = S.dma_sem("y2")
                ytmp = sb("ytmp", [128, 8, S_LOC], BF16, stack=stY)
                dma(ytmp.t[:, :, :], ys_d[:, :].rearrange("(c p) t -> p c t", p=128), d_y2, reads=[ysb], writes=[ytmp.buf])
                dump("ys", ytmp.t[:, :, :], [ytmp.buf])
                S.barrier()

        if do_out:
            with ExitStack() as stO:
                so_ = lambda name, shape, dt: sb(name, shape, dt, stack=stO)
                WOb = so_("WOb", [128, 8, D], BF16)
                k = 0
                for c in range(8):
                    t = xt[k % 2]
                    dma(t.t[:, :], wout_d[c * 128:(c + 1) * 128, :], d_xt[k % 2], writes=[t.buf])
                    S.op("dve", lambda e: e.tensor_copy(out=WOb.t[:, c, :], in_=t.t[:, :]), reads=[t.buf], writes=[WOb.buf])
                    k += 1
                YG = [so_("YG0", [128, 8, SUP], BF16), so_("YG1", [128, 8, SUP], BF16)]
                d_yg = [S.dma_sem("yg0"), S.dma_sem("yg1")]
                OB = [so_("OB0", [128, D], F32), so_("OB1", [128, D], F32)]
                d_ob = [S.dma_sem("ob0"), S.dma_sem("ob1")]
                outb = Buf("out")
                for s in range(n_super):
                    t0 = s * SUP
                    yg = YG[s % 2]
                    dma(yg.t[:, :, :], ys_d[:, t0:t0 + SUP].rearrange("(c p) t -> p c t", p=128), d_yg[s % 2], reads=[ysb], writes=[yg.buf])
                    for tt in range(4):
                        r0 = t0 + tt * 128
                        i2 = (4 * s + tt) % 2
                        X = xt[i2]
                        dma(X.t[:, :], x_d[r0:r0 + 128, :], d_xt[i2], writes=[X.buf])
                        for half in range(2):
                            P = next_pA()
                            for c in range(8):
                                S.op("pe", lambda e: e.matmul(P.t[:, :], lhsT=yg.t[:, c, tt * 128:(tt + 1) * 128], rhs=WOb.t[:, c, half * 512:(half + 1) * 512],
                                                              start=(c == 0), stop=(c == 7)), reads=[yg.buf, WOb.buf], writes=[P.buf])
                            S.op("dve", lambda e: e.tensor_tensor(out=OB[i2].t[:, half * 512:(half + 1) * 512], in0=P.t[:, :],
                                                                  in1=X.t[:, half * 512:(half + 1) * 512], op=ALU.add),
                                 reads=[P.buf, X.buf], writes=[OB[i2].buf])
                        dma(out_d[r0:r0 + 128, :], OB[i2].t[:, :], d_ob[i2], reads=[OB[i2].buf], writes=[outb], eng="act")
                for dsm in d_ob:
                    if dsm.count:
                        nc.sync.wait_ge(dsm.h, dsm.count * 16)
                S.barrier()
        if d_dbg.count:
            nc.sync.wait_ge(d_dbg.h, d_dbg.count * 16)
        if d_ys.count:
            nc.sync.wait_ge(d_ys.h, d_ys.count * 16)
    return nc


def make_in_maps(x, rel_bias, norm_w, w_in, q_norm_w, k_norm_w, conv_w, a_log, dt_bias, dn_norm_w, w_out, cores=range(4)):
    cf32, cbf, onehot, sel = host_constants()
    f32 = np.float32
    x = np.asarray(x, dtype=f32)
    w_in0 = np.asarray(w_in, dtype=f32)[0]
    w_out0 = np.ascontiguousarray(np.asarray(w_out, dtype=f32)[0])
    conv0 = np.asarray(conv_w, dtype=f32)[0]
    w_in_c = np.zeros((3, D, NCOL), dtype=f32)
    vecs = np.zeros((3, 128, 72), dtype=f32)
    relb = np.empty((2, 32, 4), dtype=f32)
    nw = np.asarray(norm_w, dtype=f32)[0].reshape(8, 128).T
    for hh in range(2):
        cols = np.concatenate([np.arange(256 * hh, 256 * hh + 256) + base for base in (0, 512, 1024, 1536)])
        w_in_c[hh, :, 0:1024] = w_in0[:, cols]
        vecs[hh, :, 0:8] = nw
        vecs[hh, 0:64, 8] = np.asarray(q_norm_w, dtype=f32)[0]
        vecs[hh, 0:64, 9] = np.asarray(k_norm_w, dtype=f32)[0]
        relb[hh] = np.asarray(rel_bias, dtype=f32)[:, 4 * hh:4 * hh + 4]
    vecs[2, :, 0:8] = nw
    for g in range(2):
        cols = np.concatenate([np.arange(256 * g, 256 * g + 256) + base for base in (2048, 2560, 3072, 3584)]
                              + [np.array([4096 + 2 * g, 4097 + 2 * g, 4100 + 2 * g, 4101 + 2 * g])])
        w_in_c[2, :, GCOL * g:GCOL * g + GCOL] = w_in0[:, cols]
        for cc in range(6):
            ch0 = (cc // 2) * 512 + 256 * g + (cc % 2) * 128
            vecs[2, :, 8 + 24 * g + cc * 4:8 + 24 * g + cc * 4 + 4] = conv0[:, ch0:ch0 + 128].T
    vecs[2, :, 56:60] = np.asarray(a_log, dtype=f32)[0][None, :]
    vecs[2, :, 60:64] = np.asarray(dt_bias, dtype=f32)[0][None, :]
    vecs[2, :, 64] = np.asarray(dn_norm_w, dtype=f32)[0]
    in_maps = []
    for b in cores:
        in_maps.append({
            "x": np.ascontiguousarray(x[b]),
            "w_in": w_in_c, "w_out": w_out0, "vecs": vecs, "relb": relb,
            "cf32": cf32, "cbf": cbf, "onehot": onehot, "sel": sel,
        })
    return in_maps


def kernel(x, rel_bias, norm_w, w_in, q_norm_w, k_norm_w, conv_w, a_log, dt_bias, dn_norm_w, w_out):
    in_maps = make_in_maps(x, rel_bias, norm_w, w_in, q_norm_w, k_norm_w, conv_w, a_log, dt_bias, dn_norm_w, w_out)
    nc = build_program(n_super=S_TOT // SUP)
    res = run_bass_kernel_spmd(nc, in_maps, core_ids=list(range(4)))
    out = np.empty((4, S_TOT, D), dtype=np.float32)
    for b in range(4):
        out[b] = res.results[b]["out"]
    return out
```
